# Optimizing a Trainium2 kernel written in Bass

```python
import jax, jax.numpy as jnp
from jax import lax
import numpy as np

D_MODEL = 4096
BATCH = 4
SEQ = 4096
DEPTH = 2

CHUNK = 64
Q_BLOCK = 128
RET_HEADS = 8
RET_DK = 128
RET_DV = 256
FOX_HEADS = 8
FOX_DH = 128
MLA_HEADS = 8
MLA_Q_LORA = 1024
MLA_KV_LORA = 512
MLA_NOPE = 128
MLA_ROPE = 64
MLA_DV = 128
D_FF = 2 * D_MODEL
N_BRANCH = 3
ROPE_BASE = 10000.0
EPS = 1e-6
IN_WIDTH = (2 * RET_HEADS * RET_DK + 2 * RET_HEADS * RET_DV
            + 3 * FOX_HEADS * FOX_DH + FOX_HEADS
            + MLA_Q_LORA + MLA_KV_LORA + MLA_ROPE)

kernel_name = "hybrid_retention_fox_mla_macaron"


def rms_norm(x, g):
    xf = x.astype(jnp.float32)
    y = xf * lax.rsqrt(jnp.mean(xf * xf, axis=-1, keepdims=True) + EPS)
    return (y * g.astype(jnp.float32)).astype(x.dtype)


def swiglu(u, w13, w2):
    a, b = jnp.split(u @ w13, 2, axis=-1)
    return (jax.nn.silu(a) * b) @ w2


def rope(t, pos):
    half = t.shape[-1] // 2
    inv = ROPE_BASE ** (-jnp.arange(half, dtype=jnp.float32) / half)
    ang = pos.astype(jnp.float32)[..., None] * inv
    cos = jnp.cos(ang)[:, :, None, :]
    sin = jnp.sin(ang)[:, :, None, :]
    t1 = t[..., :half].astype(jnp.float32)
    t2 = t[..., half:].astype(jnp.float32)
    return jnp.concatenate([t1 * cos - t2 * sin, t2 * cos + t1 * sin], axis=-1).astype(t.dtype)


def retention(q, k, v):
    b_, s_, h_, dk = q.shape
    n_chunks = s_ // CHUNK
    log_g = jnp.log1p(-jnp.exp2(-5.0 - jnp.arange(h_, dtype=jnp.float32)))
    idx = jnp.arange(CHUNK, dtype=jnp.float32)
    d_intra = jnp.exp(log_g[:, None, None] * jnp.abs(idx[:, None] - idx[None, :]))
    xi = jnp.exp(log_g[:, None] * (idx + 1.0))[None, :, :, None]
    zeta = jnp.exp(log_g[:, None] * (CHUNK - 1.0 - idx))[None, :, :, None]
    g_chunk = jnp.exp(log_g * CHUNK)[None, :, None, None]

    def to_chunks(t):
        return t.astype(jnp.float32).reshape(b_, n_chunks, CHUNK, h_, -1).transpose(1, 0, 3, 2, 4)

    qc = to_chunks(q) * (dk ** -0.5)
    kc = to_chunks(k)
    vc = to_chunks(v)

    def step(state, inp):
        qi, ki, vi = inp
        scores = jnp.einsum('bhid,bhjd->bhij', qi, ki) * d_intra
        out = (jnp.einsum('bhij,bhje->bhie', scores, vi)
               + jnp.einsum('bhid,bhde->bhie', qi * xi, state))
        state = state * g_chunk + jnp.einsum('bhjd,bhje->bhde', ki * zeta, vi)
        return state, out

    state0 = jnp.zeros((b_, h_, dk, v.shape[-1]), jnp.float32)
    _, o = lax.scan(step, state0, (qc, kc, vc))
    return o.transpose(1, 0, 3, 2, 4).reshape(b_, s_, h_, -1)


def causal_mask(tq, tk):
    return tk <= tq


def chunk_mask(tq, tk):
    return (tk // CHUNK) <= (tq // CHUNK)


def blocked_attention(q, k, v, scale, mask_fn, f_cum=None):
    s_ = q.shape[1]
    outs = []
    for i in range(s_ // Q_BLOCK):
        lo, hi = i * Q_BLOCK, (i + 1) * Q_BLOCK
        s = jnp.einsum('bqhd,bkhd->bhqk', q[:, lo:hi].astype(jnp.float32),
                       k[:, :hi].astype(jnp.float32)) * scale
        if f_cum is not None:
            fq = jnp.transpose(f_cum[:, lo:hi], (0, 2, 1))[:, :, :, None]
            fk = jnp.transpose(f_cum[:, :hi], (0, 2, 1))[:, :, None, :]
            s = s + (fq - fk)
        tq = lo + jnp.arange(Q_BLOCK)
        tk = jnp.arange(hi)
        s = jnp.where(mask_fn(tq[:, None], tk[None, :]), s, jnp.finfo(jnp.float32).min)
        p = jax.nn.softmax(s, axis=-1)
        outs.append(jnp.einsum('bhqk,bkhd->bqhd', p, v[:, :hi].astype(jnp.float32)))
    return jnp.concatenate(outs, axis=1)


def hybrid_mixer(u, positions, w_in, b_forget, ret_norm, mla_q_norm, mla_kv_norm,
                 w_uq, w_ukv, w_up_ret, w_up_fox, w_up_mla, w_gate, b_gate, w_out):
    b_, s_, _ = u.shape
    widths = [RET_HEADS * RET_DK, RET_HEADS * RET_DK, RET_HEADS * RET_DV, RET_HEADS * RET_DV,
              FOX_HEADS * FOX_DH, FOX_HEADS * FOX_DH, FOX_HEADS * FOX_DH, FOX_HEADS,
              MLA_Q_LORA, MLA_KV_LORA, MLA_ROPE]
    splits = np.cumsum(widths)[:-1].tolist()
    z = u @ w_in
    rq, rk, rv, rg, fq, fk, fv, ff, cq, ckv, kr = jnp.split(z, splits, axis=-1)

    rq = rope(rq.reshape(b_, s_, RET_HEADS, RET_DK), positions)
    rk = rope(rk.reshape(b_, s_, RET_HEADS, RET_DK), positions)
    ro = retention(rq, rk, rv.reshape(b_, s_, RET_HEADS, RET_DV))
    ro = ro * lax.rsqrt(jnp.mean(ro * ro, axis=-1, keepdims=True) + EPS)
    ro = ro * ret_norm.astype(jnp.float32).reshape(RET_HEADS, RET_DV)
    ro = ro * jax.nn.silu(rg.astype(jnp.float32).reshape(b_, s_, RET_HEADS, RET_DV))
    ro = ro.reshape(b_, s_, RET_HEADS * RET_DV).astype(u.dtype)

    log_f = jax.nn.log_sigmoid(ff.astype(jnp.float32) + b_forget.astype(jnp.float32))
    f_cum = jnp.cumsum(log_f, axis=1)
    fo = blocked_attention(fq.reshape(b_, s_, FOX_HEADS, FOX_DH),
                           fk.reshape(b_, s_, FOX_HEADS, FOX_DH),
                           fv.reshape(b_, s_, FOX_HEADS, FOX_DH),
                           FOX_DH ** -0.5, causal_mask, f_cum)
    fo = fo.reshape(b_, s_, FOX_HEADS * FOX_DH).astype(u.dtype)

    qf = (rms_norm(cq, mla_q_norm) @ w_uq).reshape(b_, s_, MLA_HEADS, MLA_NOPE + MLA_ROPE)
    q_nope, q_pe = qf[..., :MLA_NOPE], rope(qf[..., MLA_NOPE:], positions)
    kvf = (rms_norm(ckv, mla_kv_norm) @ w_ukv).reshape(b_, s_, MLA_HEADS, MLA_NOPE + MLA_DV)
    k_nope, mv = kvf[..., :MLA_NOPE], kvf[..., MLA_NOPE:]
    k_pe = rope(kr[:, :, None, :], positions)
    mq = jnp.concatenate([q_nope, q_pe], axis=-1)
    mk = jnp.concatenate([k_nope, jnp.broadcast_to(k_pe, (b_, s_, MLA_HEADS, MLA_ROPE))], axis=-1)
    mo = blocked_attention(mq, mk, mv, (MLA_NOPE + MLA_ROPE) ** -0.5, chunk_mask)
    mo = mo.reshape(b_, s_, MLA_HEADS * MLA_DV).astype(u.dtype)

    merged = (jax.nn.sigmoid(u @ w_gate[0] + b_gate[0]) * (ro @ w_up_ret)
              + jax.nn.sigmoid(u @ w_gate[1] + b_gate[1]) * (fo @ w_up_fox)
              + jax.nn.sigmoid(u @ w_gate[2] + b_gate[2]) * (mo @ w_up_mla))
    return merged @ w_out


def setup_inputs(seed: int = 0) -> dict:
    key = jax.random.key(seed)
    ks = jax.random.split(key, 24)
    f32 = jnp.float32

    def dense(k, shape):
        return jax.random.normal(k, shape, f32) * (shape[-2] ** -0.5)

    def gain(k, shape):
        return 1.0 + 0.02 * jax.random.normal(k, shape, f32)

    x = jax.random.normal(ks[0], (BATCH, SEQ, D_MODEL), f32)
    positions = (jax.random.randint(ks[1], (BATCH, 1), 0, 4096, dtype=jnp.int32)
                 + jnp.arange(SEQ, dtype=jnp.int32)[None, :])
    return {
        "x": x,
        "positions": positions,
        "ffn1_norm": gain(ks[2], (DEPTH, D_MODEL)),
        "ffn1_w13": dense(ks[3], (DEPTH, D_MODEL, 2 * D_FF)),
        "ffn1_w2": dense(ks[4], (DEPTH, D_FF, D_MODEL)),
        "mix_norm": gain(ks[5], (DEPTH, D_MODEL)),
        "w_in": dense(ks[6], (DEPTH, D_MODEL, IN_WIDTH)),
        "b_forget": 2.0 + 0.1 * jax.random.normal(ks[7], (DEPTH, FOX_HEADS), f32),
        "ret_norm": gain(ks[8], (DEPTH, RET_HEADS * RET_DV)),
        "mla_q_norm": gain(ks[9], (DEPTH, MLA_Q_LORA)),
        "mla_kv_norm": gain(ks[10], (DEPTH, MLA_KV_LORA)),
        "w_uq": dense(ks[11], (DEPTH, MLA_Q_LORA, MLA_HEADS * (MLA_NOPE + MLA_ROPE))),
        "w_ukv": dense(ks[12], (DEPTH, MLA_KV_LORA, MLA_HEADS * (MLA_NOPE + MLA_DV))),
        "w_up_ret": dense(ks[13], (DEPTH, RET_HEADS * RET_DV, D_MODEL)),
        "w_up_fox": dense(ks[14], (DEPTH, FOX_HEADS * FOX_DH, D_MODEL)),
        "w_up_mla": dense(ks[15], (DEPTH, MLA_HEADS * MLA_DV, D_MODEL)),
        "w_gate": dense(ks[16], (DEPTH, N_BRANCH, D_MODEL, D_MODEL)),
        "b_gate": 0.02 * jax.random.normal(ks[17], (DEPTH, N_BRANCH, D_MODEL), f32),
        "w_out": dense(ks[18], (DEPTH, D_MODEL, D_MODEL)),
        "ffn2_norm": gain(ks[19], (DEPTH, D_MODEL)),
        "ffn2_w13": dense(ks[20], (DEPTH, D_MODEL, 2 * D_FF)),
        "ffn2_w2": dense(ks[21], (DEPTH, D_FF, D_MODEL)),
        "final_norm": gain(ks[22], (D_MODEL,)),
    }


def reference(x, positions, ffn1_norm, ffn1_w13, ffn1_w2, mix_norm, w_in, b_forget,
              ret_norm, mla_q_norm, mla_kv_norm, w_uq, w_ukv, w_up_ret, w_up_fox,
              w_up_mla, w_gate, b_gate, w_out, ffn2_norm, ffn2_w13, ffn2_w2, final_norm):
    h = x
    for l in range(DEPTH):
        h = h + 0.5 * swiglu(rms_norm(h, ffn1_norm[l]), ffn1_w13[l], ffn1_w2[l])
        h = h + hybrid_mixer(rms_norm(h, mix_norm[l]), positions, w_in[l], b_forget[l],
                             ret_norm[l], mla_q_norm[l], mla_kv_norm[l], w_uq[l], w_ukv[l],
                             w_up_ret[l], w_up_fox[l], w_up_mla[l], w_gate[l], b_gate[l],
                             w_out[l])
        h = h + 0.5 * swiglu(rms_norm(h, ffn2_norm[l]), ffn2_w13[l], ffn2_w2[l])
    return rms_norm(h, final_norm)
```

```python
from contextlib import ExitStack
import numpy as np
import ml_dtypes
import concourse.bass as bass
import concourse.mybir as mybir
from concourse.bass_utils import run_bass_kernel_spmd

F32 = mybir.dt.float32
BF16 = mybir.dt.bfloat16
I32 = mybir.dt.int32
AF = mybir.ActivationFunctionType
ALU = mybir.AluOpType

D = 4096
DFF = 8192
T = 2048
S = 4096
TT = 512
NT = T // TT
KC = D // 128
EPS = 1e-6
INW = 10824
C_RQ, C_RK, C_RV, C_RG = 0, 1024, 2048, 4096
C_FQ, C_FK, C_FV, C_FF = 6144, 7168, 8192, 9216
C_CQ, C_CKV, C_KR = 9224, 10248, 10760
SLAB = 8192
NSLAB = 4


class Tok:
    __slots__ = ("sem", "v")

    def __init__(self, sem, v):
        self.sem = sem
        self.v = v


def _flat(xs):
    for x in xs:
        if x is None:
            continue
        if isinstance(x, (list, tuple)):
            yield from _flat(x)
        else:
            yield x


class Eng:
    def __init__(self, k, name, eng):
        self.k = k
        self.name = name
        self.eng = eng
        self.sem = k.new_sem("e_" + name)
        self.cnt = 0
        self.seen = {}
        self.last = None
        self.last_tok = None

    def __call__(self, ins):
        self.last = ins
        self.last_tok = None
        return ins

    def m(self, ins):
        ins.then_inc(self.sem, 1)
        self.cnt += 1
        self.last = ins
        self.last_tok = Tok(self.sem, self.cnt)
        return self.last_tok

    def wait(self, *toks):
        for t in _flat(toks):
            if t.sem is self.sem:
                continue
            key = id(t.sem)
            if self.seen.get(key, -1) >= t.v:
                continue
            self.eng.wait_ge(t.sem, t.v)
            self.seen[key] = t.v

    def sw(self, tok):
        self.eng.wait_ge(tok.sem, tok.v)

    def tail(self):
        if self.last is None:
            return None
        if self.last_tok is None:
            self.m(self.last)
        return self.last_tok


class _DSem:
    def __init__(self, k, name):
        self.sem = k.new_sem("d_" + name)
        self.cnt = 0
        k.dsems.append(self)

    def dma(self, q, out, in_, **kw):
        ins = q.eng.dma_start(out=out, in_=in_, **kw)
        ins.then_inc(self.sem, 16)
        self.cnt += 16
        return Tok(self.sem, self.cnt)

    def tok(self):
        return Tok(self.sem, self.cnt) if self.cnt else None


def DSem(k, name):
    if k.ds_pool:
        d = k.ds_pool.pop()
    else:
        d = _DSem(k, name)
    if k.phase_es is not None:
        k.phase_ds.append(d)
    return d


class Ring:
    def __init__(self, k, name, n, shape, dtype, dma=True, psum=False):
        self.n = n
        self.i = 0
        self.bufs = []
        self.ds = []
        self.free = [None] * n
        for j in range(n):
            if psum:
                self.bufs.append(k.psum(f"{name}{j}", shape, dtype))
            else:
                self.bufs.append(k.sbuf(f"{name}{j}", shape, dtype))
            self.ds.append(DSem(k, f"{name}{j}") if dma else None)

    def next(self):
        j = self.i % self.n
        self.i += 1
        return j


class K:
    def __init__(self):
        self.nc = bass.Bass("TRN2", target_bir_lowering=False)
        self.es = ExitStack()
        self.phase_es = None
        self.dsems = []
        self.ds_pool = []
        self.phase_ds = []
        self.nsem = 0
        self.uid = 0
        nc = self.nc
        self.pe = Eng(self, "pe", nc.tensor)
        self.act = Eng(self, "act", nc.scalar)
        self.dve = Eng(self, "dve", nc.vector)
        self.pool = Eng(self, "pool", nc.gpsimd)
        self.sp = Eng(self, "sp", nc.sync)
        self.engs = [self.pe, self.act, self.dve, self.pool, self.sp]
        self.dram = {}
        self.in_names = []
        self.out_names = []

    def new_sem(self, name):
        self.nsem += 1
        return self.es.enter_context(self.nc.semaphore(name))

    def sbuf(self, name, shape, dtype):
        st = self.phase_es if self.phase_es is not None else self.es
        self.uid += 1
        return st.enter_context(self.nc.sbuf_tensor(f"{name}_{self.uid}", list(shape), dtype))

    def psum(self, name, shape, dtype):
        st = self.phase_es if self.phase_es is not None else self.es
        self.uid += 1
        return st.enter_context(self.nc.psum_tensor(f"{name}_{self.uid}", list(shape), dtype))

    def dt(self, name, shape, dtype, kind):
        t = self.nc.dram_tensor(name, list(shape), dtype, kind=kind).ap()
        self.dram[name] = t
        if kind == "ExternalInput":
            self.in_names.append(name)
        elif kind == "ExternalOutput":
            self.out_names.append(name)
        return t

    def barrier(self):
        toks = [e.tail() for e in self.engs]
        toks += [d.tok() for d in self.dsems]
        for e in self.engs:
            e.wait(toks)

    def begin_phase(self):
        self.phase_es = ExitStack()

    def end_phase(self):
        self.barrier()
        self.phase_es.close()
        self.phase_es = None
        self.ds_pool.extend(self.phase_ds)
        self.phase_ds = []


class Common:
    def __init__(self, k, cst):
        nc = k.nc
        self.k = k
        self.ident = k.sbuf("ident", [128, 128], F32)
        self.ones_f = k.sbuf("ones_f", [128, 128], F32)
        self.ones_b = k.sbuf("ones_b", [128, 128], BF16)
        self.ident_b = k.sbuf("ident_b", [128, 128], BF16)
        ds = DSem(k, "cst")
        t = ds.dma(k.sp, self.ident[:, :], cst["ident"][:, :])
        k.dve.wait(t)
        k.dve(nc.vector.memset(self.ones_f[:, :], 1.0))
        k.dve(nc.vector.memset(self.ones_b[:, :], 1.0))
        k.dve(nc.vector.tensor_copy(out=self.ident_b[:, :], in_=self.ident[:, :]))
        self.cst_tok = k.dve.tail()
        for e in (k.pe, k.act):
            e.wait(self.cst_tok)
        self.ps = [k.psum(f"ps{i}", [128, 512], F32) for i in range(8)]
        self.ps_free = [None] * 8
        self.wr = Ring(k, "wsl", NSLAB, [128, SLAB], BF16)

    def wload(self, pieces):
        k = self.k
        j = self.wr.next()
        tl = self.wr.bufs[j]
        k.pool.wait(self.wr.free[j])
        tok = None
        for dst_fn, src in pieces:
            tok = self.wr.ds[j].dma(k.pool, dst_fn(tl), src, max_dma_last_dim=8192)
        return j, tl, tok

    def wfree(self, j, tok):
        self.wr.free[j] = tok


def wview(w2d, kc0, nkc, c0, ncol):
    v = w2d.rearrange("(c p) n -> p c n", p=128)
    return v[:, kc0:kc0 + nkc, c0:c0 + ncol]


def slab_view(tl, nkc, ncol):
    return tl[:, 0:nkc * ncol].rearrange("p (c n) -> p c n", n=ncol)


def norm_u(k, cm, hT, t, gain_sb, uT, stage, out_f32=None):
    nc = k.nc
    pe, act, dve, sp = k.pe, k.act, k.dve, k.sp
    cols = slice(t * TT, (t + 1) * TT)
    PSS = 7
    pe.wait(cm.ps_free[PSS])
    sq = norm_u.sq
    last_mm = None
    for c in range(KC):
        j = stage.next()
        sp.wait(stage.free[j])
        tk = stage.ds[j].dma(sp, stage.bufs[j][:, :], hT[c * 128:(c + 1) * 128, cols])
        js = sq.next()
        act.wait(tk, sq.free[js])
        ta = act.m(nc.scalar.activation(out=sq.bufs[js][:, :], in_=stage.bufs[j][:, :], func=AF.Square))
        stage.free[j] = ta
        pe.wait(ta)
        last_mm = nc.tensor.matmul(cm.ps[PSS][:, :], lhsT=cm.ones_f[:, :], rhs=sq.bufs[js][:, :],
                                   start=(c == 0), stop=(c == KC - 1))
        sq.free[js] = pe.m(last_mm)
    tss = pe.last_tok
    rstd = norm_u.rstd
    act.wait(tss, norm_u.rstd_free)
    ta = act.m(nc.scalar.activation(out=rstd[:, :], in_=cm.ps[PSS][:, :], func=AF.Sqrt,
                                    scale=1.0 / D, bias=norm_u.eps_t[:, 0:1]))
    cm.ps_free[PSS] = ta
    dve.wait(ta)
    dve(nc.vector.reciprocal(out=rstd[:, :], in_=rstd[:, :]))
    last = None
    for c in range(KC):
        j = stage.next()
        sp.wait(stage.free[j])
        tk = stage.ds[j].dma(sp, stage.bufs[j][:, :], hT[c * 128:(c + 1) * 128, cols])
        dve.wait(tk)
        if out_f32 is None:
            if c == 0:
                dve.wait(norm_u.u_free)
            last = dve.m(nc.vector.scalar_tensor_tensor(
                out=uT[:, c, :], in0=stage.bufs[j][:, :], scalar=gain_sb[:, c:c + 1], in1=rstd[:, :],
                op0=ALU.mult, op1=ALU.mult))
            stage.free[j] = last
        else:
            last = out_f32(c, stage.bufs[j], rstd)
            stage.free[j] = last
    norm_u.rstd_free = last
    return last


def setup_norm(k):
    norm_u.sq = Ring(k, "nsq", 2, [128, TT], F32, dma=False)
    norm_u.rstd = k.sbuf("rstd", [128, TT], F32)
    norm_u.eps_t = k.sbuf("eps_t", [128, 1], F32)
    k.dve(k.nc.vector.memset(norm_u.eps_t[:, :], EPS))
    k.act.wait(k.dve.tail())
    norm_u.rstd_free = None
    norm_u.u_free = None


def phase_p0(k, cm, x, hT):
    nc = k.nc
    pe, act, dve, sp = k.pe, k.act, k.dve, k.sp
    k.begin_phase()
    xs = Ring(k, "p0x", 8, [128, D], F32)
    st = Ring(k, "p0s", 4, [128, TT], F32)
    pi = 0
    for t in range(NT):
        xt = []
        for b in range(4):
            j = xs.next()
            sp.wait(xs.free[j])
            r0 = t * TT + b * 128
            tk = xs.ds[j].dma(sp, xs.bufs[j][:, :], x[r0:r0 + 128, :])
            xt.append((j, tk))
        for c in range(KC):
            pb = pi % 4
            pi += 1
            pe.wait(cm.ps_free[pb])
            for b in range(4):
                j, tk = xt[b]
                pe.wait(tk)
                mm = nc.tensor.matmul(cm.ps[pb][:, b * 128:(b + 1) * 128],
                                      lhsT=xs.bufs[j][:, c * 128:(c + 1) * 128], rhs=cm.ident[:, :],
                                      start=True, stop=True)
            tp = pe.m(mm)
            if c == KC - 1:
                for b in range(4):
                    xs.free[xt[b][0]] = tp
            js = st.next()
            eng, e = (act, nc.scalar) if c % 2 == 0 else (dve, nc.vector)
            eng.wait(tp, st.ds[js].tok())
            if eng is act:
                te = act.m(nc.scalar.copy(out=st.bufs[js][:, :], in_=cm.ps[pb][:, :]))
            else:
                te = dve.m(nc.vector.tensor_copy(out=st.bufs[js][:, :], in_=cm.ps[pb][:, :]))
            cm.ps_free[pb] = te
            sp.wait(te)
            st.ds[js].dma(sp, hT[c * 128:(c + 1) * 128, t * TT:(t + 1) * TT], st.bufs[js][:, :])
    k.end_phase()


def phase_ffn(k, cm, hT, gain, w13, w2):
    nc = k.nc
    pe, act, dve, sp = k.pe, k.act, k.dve, k.sp
    k.begin_phase()
    setup_norm(k)
    TW = 2 * TT
    HQ = 4
    HC = DFF // 128 // HQ
    g_sb = k.sbuf("ffn_g", [128, KC], F32)
    gds = DSem(k, "ffn_g")
    dve.wait(gds.dma(sp, g_sb[:, :], gain[:, :]))
    uT = k.sbuf("ffn_u", [128, KC, TW], BF16)
    hid = k.sbuf("ffn_hid", [128, HC, TW], BF16)
    stage = Ring(k, "ffn_st", 4, [128, TT], F32)
    sa = Ring(k, "ffn_sa", 2, [128, TT], F32, dma=False)
    res = Ring(k, "ffn_res", 4, [128, TT], F32)
    hid_free = None
    pi = 0
    for t in range(T // TW):
        for sub in range(2):
            tu = norm_u(k, cm, hT, t * 2 + sub, g_sb, uT[:, :, sub * TT:(sub + 1) * TT], stage)
        pe.wait(tu)
        wtok = {}
        for q in range(HQ):
            for s in range(HC // 2):
                ca = q * HC * 128 + s * 256
                ja, ta_, tka = cm.wload([(lambda tl: slab_view(tl, KC, 256), wview(w13, 0, KC, ca, 256))])
                jb, tb_, tkb = cm.wload([(lambda tl: slab_view(tl, KC, 256), wview(w13, 0, KC, DFF + ca, 256))])
                va = slab_view(ta_, KC, 256)
                vb = slab_view(tb_, KC, 256)
                pe.wait(tka, tkb)
                for cc in range(2):
                    n = s * 2 + cc
                    for sub in range(2):
                        usl = slice(sub * TT, (sub + 1) * TT)
                        pa = (pi % 2) * 2
                        pb = pa + 1
                        pi += 1
                        pe.wait(cm.ps_free[pa], cm.ps_free[pb])
                        for kc in range(KC):
                            mm = nc.tensor.matmul(cm.ps[pa][:, :], lhsT=va[:, kc, cc * 128:(cc + 1) * 128], rhs=uT[:, kc, usl],
                                                  start=(kc == 0), stop=(kc == KC - 1))
                        tpa = pe.m(mm)
                        for kc in range(KC):
                            mm = nc.tensor.matmul(cm.ps[pb][:, :], lhsT=vb[:, kc, cc * 128:(cc + 1) * 128], rhs=uT[:, kc, usl],
                                                  start=(kc == 0), stop=(kc == KC - 1))
                        tpb = pe.m(mm)
                        js = sa.next()
                        act.wait(tpa, sa.free[js])
                        tsa = act.m(nc.scalar.activation(out=sa.bufs[js][:, :], in_=cm.ps[pa][:, :], func=AF.Silu))
                        cm.ps_free[pa] = tsa
                        dve.wait(tsa, tpb)
                        if n == 0 and sub == 0:
                            dve.wait(hid_free)
                        th = dve.m(nc.vector.tensor_tensor(out=hid[:, n, usl], in0=cm.ps[pb][:, :], in1=sa.bufs[js][:, :],
                                                           op=ALU.mult))
                        sa.free[js] = th
                        cm.ps_free[pb] = th
                cm.wfree(ja, tpb)
                cm.wfree(jb, tpb)
            if q == HQ - 1:
                norm_u.u_free = pe.last_tok
            pe.wait(k.dve.last_tok)
            for s in range(D // 512):
                j_, tl_, tk_ = cm.wload([(lambda tl: slab_view(tl, HC, 512), wview(w2, q * HC, HC, s * 512, 512))])
                v_ = slab_view(tl_, HC, 512)
                pe.wait(tk_)
                for cc in range(4):
                    n = s * 4 + cc
                    for sub in range(2):
                        usl = slice(sub * TT, (sub + 1) * TT)
                        csl = slice(t * TW + sub * TT, t * TW + (sub + 1) * TT)
                        pb_ = pi % 4
                        pi += 1
                        pe.wait(cm.ps_free[pb_])
                        for kc in range(HC):
                            mm = nc.tensor.matmul(cm.ps[pb_][:, :], lhsT=v_[:, kc, cc * 128:(cc + 1) * 128], rhs=hid[:, kc, usl],
                                                  start=(kc == 0), stop=(kc == HC - 1))
                        tcc = pe.m(mm)
                        j = stage.next()
                        sp.wait(stage.free[j], wtok.get((n, sub)))
                        tk = stage.ds[j].dma(sp, stage.bufs[j][:, :], hT[n * 128:(n + 1) * 128, csl])
                        jr = res.next()
                        dve.wait(tk, tcc, res.ds[jr].tok())
                        tr = dve.m(nc.vector.scalar_tensor_tensor(
                            out=res.bufs[jr][:, :], in0=cm.ps[pb_][:, :], scalar=0.5, in1=stage.bufs[j][:, :],
                            op0=ALU.mult, op1=ALU.add))
                        stage.free[j] = tr
                        cm.ps_free[pb_] = tr
                        sp.wait(tr)
                        wtok[(n, sub)] = res.ds[jr].dma(sp, hT[n * 128:(n + 1) * 128, csl], res.bufs[jr][:, :])
                cm.wfree(j_, tcc)
            hid_free = pe.last_tok
    k.end_phase()


_bank_rot = [0]


def linear_fm(k, cm, src, nkc, w2d, c0, ncols, epi, tw=TT, banks=(0, 1, 2, 3), sw=256, lhs_fn=None):
    nc = k.nc
    pe = k.pe
    for s0 in range(0, ncols, sw):
        w = min(sw, ncols - s0)
        j, tl, tk = cm.wload([(lambda tl_: slab_view(tl_, nkc, w), wview(w2d, 0, nkc, c0 + s0, w))])
        v = slab_view(tl, nkc, w)
        pe.wait(tk)
        tp = None
        for cc in range(w // 128):
            b = banks[_bank_rot[0] % len(banks)]
            _bank_rot[0] += 1
            pe.wait(cm.ps_free[b])
            for kc in range(nkc):
                mm = nc.tensor.matmul(cm.ps[b][:, 0:tw], lhsT=v[:, kc, cc * 128:(cc + 1) * 128], rhs=src(kc),
                                      start=(kc == 0), stop=(kc == nkc - 1))
            tp = pe.m(mm)
            cm.ps_free[b] = epi(s0 // 128 + cc, cm.ps[b], tp)
        cm.wfree(j, tp)


def linear_tm(k, cm, uT, nkc, wsrc, ncols, epi, rhs_fn=None, slab_cols=None):
    nc = k.nc
    pe = k.pe
    sc = slab_cols or ncols
    kg = min(nkc, SLAB // sc)
    for tb in range(4):
        pe.wait(cm.ps_free[tb])
    tp = None
    ng = nkc // kg
    for g in range(ng):
        j, tl, tk = cm.wload([(lambda tl_: slab_view(tl_, kg, sc), wsrc(g * kg, kg))])
        v = slab_view(tl, kg, sc)
        pe.wait(tk)
        for tb in range(4):
            for kc in range(kg):
                rhs = v[:, kc, 0:ncols] if rhs_fn is None else rhs_fn(tl, kg, kc)
                mm = nc.tensor.matmul(cm.ps[tb][:, 0:ncols], lhsT=uT[:, g * kg + kc, tb * 128:(tb + 1) * 128], rhs=rhs,
                                      start=(g == 0 and kc == 0), stop=(g == ng - 1 and kc == kg - 1))
        tp = pe.m(mm)
        cm.wfree(j, tp)
    for tb in range(4):
        cm.ps_free[tb] = epi(tb, cm.ps[tb], tp)


class OutStage:
    def __init__(self, k, name, n, shape, dtype):
        self.k = k
        self.r = Ring(k, name, n, shape, dtype)
        self.flip = 0

    def slot(self, eng):
        j = self.r.next()
        eng.wait(self.r.ds[j].tok())
        return j, self.r.bufs[j]

    def store(self, j, tok, pieces):
        self.k.sp.wait(tok)
        for dst, src in pieces:
            self.r.ds[j].dma(self.k.sp, dst, src)

    def copy_epi(self, dst_fn, func=None, parts=128, tw=TT):
        k = self.k
        nc = k.nc

        def epi(n, ps, tok):
            use_act = (func is not None) or (self.flip % 2 == 0)
            self.flip += 1
            eng = k.act if use_act else k.dve
            j, buf = self.slot(eng)
            eng.wait(tok)
            if use_act and func is not None:
                te = k.act.m(nc.scalar.activation(out=buf[0:parts, 0:tw], in_=ps[0:parts, 0:tw], func=func))
            elif use_act:
                te = k.act.m(nc.scalar.copy(out=buf[0:parts, 0:tw], in_=ps[0:parts, 0:tw]))
            else:
                te = k.dve.m(nc.vector.tensor_copy(out=buf[0:parts, 0:tw], in_=ps[0:parts, 0:tw]))
            self.store(j, te, [(dst_fn(n), buf[0:parts, 0:tw])])
            return te
        return epi


PI = float(np.pi)


def phase_tables(k, cm, pos, inv, tabs):
    nc = k.nc
    act, dve, sp = k.act, k.dve, k.sp
    k.begin_phase()
    posi = k.sbuf("tb_posi", [128, T], I32)
    posf = k.sbuf("tb_posf", [128, T], F32)
    ang = k.sbuf("tb_ang", [128, T], F32)
    red = k.sbuf("tb_red", [128, T], F32)
    invs = k.sbuf("tb_inv", [128, 2], F32)
    npi = k.sbuf("tb_npi", [128, 1], F32)
    outb = Ring(k, "tb_o", 2, [128, T], F32)
    ds = DSem(k, "tb_in")
    ds.dma(sp, posi[:, :], pos.partition_broadcast(128))
    t0 = ds.dma(sp, invs[:, :], inv[:, :])
    dve.wait(t0)
    dve(nc.vector.memset(npi[:, :], -PI))
    dve(nc.vector.tensor_copy(out=posf[:, :], in_=posi[:, :]))
    ki = k.sbuf("tb_ki", [128, T], I32)
    kf = k.sbuf("tb_kf", [128, T], F32)
    C1 = 6.28125
    C2 = 2.0 * PI - 6.28125
    for ti, (col, nm_c, nm_s) in enumerate(((0, "cosR", "sinR"), (1, "cosM", "sinM"))):
        for nm, shift in ((nm_s, 0.0), (nm_c, 0.5 * PI)):
            dve(nc.vector.tensor_scalar(out=ang[:, :], in0=posf[:, :], scalar1=invs[:, col:col + 1], scalar2=shift,
                                        op0=ALU.mult, op1=ALU.add))
            dve(nc.vector.tensor_scalar(out=kf[:, :], in0=ang[:, :], scalar1=1.0 / (2.0 * PI), scalar2=None, op0=ALU.mult))
            dve(nc.vector.tensor_copy(out=ki[:, :], in_=kf[:, :]))
            dve(nc.vector.tensor_copy(out=kf[:, :], in_=ki[:, :]))
            dve(nc.vector.scalar_tensor_tensor(out=ang[:, :], in0=kf[:, :], scalar=-C1, in1=ang[:, :],
                                               op0=ALU.mult, op1=ALU.add))
            dve(nc.vector.scalar_tensor_tensor(out=ang[:, :], in0=kf[:, :], scalar=-C2, in1=ang[:, :],
                                               op0=ALU.mult, op1=ALU.add))
            dve(nc.vector.tensor_scalar(out=kf[:, :], in0=ang[:, :], scalar1=PI, scalar2=2.0 * PI,
                                        op0=ALU.is_gt, op1=ALU.mult))
            if ti or nm == nm_c:
                dve.wait(k.act.last_tok)
            td = dve.m(nc.vector.tensor_tensor(out=red[:, :], in0=ang[:, :], in1=kf[:, :], op=ALU.subtract))
            j = outb.next()
            act.wait(td, outb.ds[j].tok())
            ta = act.m(nc.scalar.activation(out=outb.bufs[j][:, :], in_=red[:, :], func=AF.Sin))
            sp.wait(ta)
            outb.ds[j].dma(sp, tabs[nm][:, :], outb.bufs[j][:, :])
    k.end_phase()


def phase_mixa(k, cm, hT, W, tabs, O):
    nc = k.nc
    pe, act, dve, sp = k.pe, k.act, k.dve, k.sp
    k.begin_phase()
    setup_norm(k)
    w_in = W["w_in"]
    g_sb = k.sbuf("ma_g", [128, KC], F32)
    bfg = k.sbuf("ma_bfg", [128, 8], F32)
    qng = k.sbuf("ma_qng", [128, 8], F32)
    kvg = k.sbuf("ma_kvg", [128, 4], F32)
    one_t = k.sbuf("ma_one", [128, 1], F32)
    pds = DSem(k, "ma_p")
    pds.dma(sp, g_sb[:, :], W["mix_g"][:, :])
    pds.dma(sp, bfg[:, :], W["bfg"][:, :])
    pds.dma(sp, qng[:, :], W["qn_g"][:, :])
    tparam = pds.dma(sp, kvg[:, :], W["kvn_g"][:, :])
    dve.wait(tparam)
    act.wait(tparam)
    dve(nc.vector.memset(one_t[:, :], 1.0))
    act.wait(dve.tail())
    uT = k.sbuf("ma_u", [128, KC, TT], BF16)
    stage = Ring(k, "ma_st", 3, [128, TT], F32)
    tabr = [k.sbuf(f"ma_tab{i}", [128, TT], F32) for i in range(4)]
    tab_ds = DSem(k, "ma_tab")
    tab_free = None
    tmp1 = k.sbuf("ma_tmp1", [128, TT], F32)
    tmp2 = k.sbuf("ma_tmp2", [128, TT], F32)
    ob = OutStage(k, "ma_ob", 6, [128, TT], BF16)
    cT = k.sbuf("ma_cT", [128, 8, TT], F32)
    cn = k.sbuf("ma_cn", [128, 8, TT], BF16)
    fls = Ring(k, "ma_fl", 2, [128, 8], F32)
    fx = k.sbuf("ma_fx", [128, 8], F32)
    fl2 = k.sbuf("ma_fl2", [128, 8], F32)
    cT_free = None
    cn_free = None

    def rope_epi(psA, psB, tA, tB, cosT, sinT, parts, stores):
        P = slice(0, parts)
        j1, b1 = ob.slot(dve)
        j2, b2 = ob.slot(dve)
        dve.wait(tA, tB)
        dve(nc.vector.tensor_tensor(out=tmp1[P, :], in0=psA[P, :], in1=cosT[P, :], op=ALU.mult))
        dve(nc.vector.tensor_tensor(out=tmp2[P, :], in0=psB[P, :], in1=sinT[P, :], op=ALU.mult))
        t1 = dve.m(nc.vector.tensor_tensor(out=b1[P, :], in0=tmp1[P, :], in1=tmp2[P, :], op=ALU.subtract))
        dve(nc.vector.tensor_tensor(out=tmp1[P, :], in0=psB[P, :], in1=cosT[P, :], op=ALU.mult))
        dve(nc.vector.tensor_tensor(out=tmp2[P, :], in0=psA[P, :], in1=sinT[P, :], op=ALU.mult))
        t2 = dve.m(nc.vector.tensor_tensor(out=b2[P, :], in0=tmp1[P, :], in1=tmp2[P, :], op=ALU.add))
        p1, p2 = stores(b1, b2)
        ob.store(j1, t1, p1)
        ob.store(j2, t2, p2)
        return t2

    rope_banks = [(4, 5), (2, 3)]
    rb = 0
    for t in range(NT):
        cols = slice(t * TT, (t + 1) * TT)
        tu = norm_u(k, cm, hT, t, g_sb, uT, stage)
        pe.wait(tu)
        sp.wait(tab_free)
        for i, nm in enumerate(("cosR", "sinR", "cosM", "sinM")):
            ttab = tab_ds.dma(sp, tabr[i][:, :], tabs[nm][:, cols])
        dve.wait(ttab)
        cosR, sinR, cosM, sinM = tabr

        def src_u(kc):
            return uT[:, kc, :]

        for dst, c_off in ((O["RQ"], C_RQ), (O["RK"], C_RK)):
            for p in range(4):
                pieces = []
                for two in range(2):
                    for hh in range(2):
                        pieces.append((lambda tl_, two=two, hh=hh: slab_view(tl_, KC, 256)[:, :, two * 128 + hh * 64:two * 128 + hh * 64 + 64],
                                       wview(w_in, 0, KC, c_off + p * 256 + hh * 128 + two * 64, 64)))
                j, tl, tk = cm.wload(pieces)
                v5 = tl[:, 0:KC * 256].rearrange("p (c two m) -> p c two m", two=2, m=128)
                bA, bB = rope_banks[rb % 2]
                rb += 1
                pe.wait(tk, cm.ps_free[bA], cm.ps_free[bB])
                for kc in range(KC):
                    mm = nc.tensor.matmul(cm.ps[bA][:, :], lhsT=v5[:, kc, 0, :], rhs=uT[:, kc, :],
                                          start=(kc == 0), stop=(kc == KC - 1))
                tA = pe.m(mm)
                for kc in range(KC):
                    mm = nc.tensor.matmul(cm.ps[bB][:, :], lhsT=v5[:, kc, 1, :], rhs=uT[:, kc, :],
                                          start=(kc == 0), stop=(kc == KC - 1))
                tB = pe.m(mm)
                cm.wfree(j, tB)

                def stores(b1, b2, p=p, dst=dst):
                    p1, p2 = [], []
                    for hh in range(2):
                        h = 2 * p + hh
                        p1.append((dst[h // 4, h % 4, 0:64, cols], b1[hh * 64:(hh + 1) * 64, :]))
                        p2.append((dst[h // 4, h % 4, 64:128, cols], b2[hh * 64:(hh + 1) * 64, :]))
                    return p1, p2
                te = rope_epi(cm.ps[bA], cm.ps[bB], tA, tB, cosR, sinR, 128, stores)
                cm.ps_free[bA] = te
                cm.ps_free[bB] = te
        for g in range(4):
            def epi_rv(tb, ps, tok, g=g):
                return ob.copy_epi(lambda n: O["RV"][g // 2, t * TT + tb * 128:t * TT + (tb + 1) * 128,
                                                     (g % 2) * 512:(g % 2) * 512 + 512])(tb, ps, tok)
            linear_tm(k, cm, uT, KC, lambda kc0, n, g=g: wview(w_in, kc0, n, C_RV + g * 512, 512), 512, epi_rv)
        linear_fm(k, cm, src_u, KC, w_in, C_RG, 2048,
                  ob.copy_epi(lambda n: O["RG"][n // 8, (n % 8) * 128:(n % 8) * 128 + 128, cols], func=AF.Silu))
        linear_fm(k, cm, src_u, KC, w_in, C_FQ, 1024, ob.copy_epi(lambda n: O["FQ"][n // 4, n % 4, :, cols]))
        linear_fm(k, cm, src_u, KC, w_in, C_FK, 1024, ob.copy_epi(lambda n: O["FK"][n // 4, n % 4, :, cols]))
        for g in range(2):
            def epi_fv(tb, ps, tok, g=g):
                return ob.copy_epi(lambda n: O["FV"][g, t * TT + tb * 128:t * TT + (tb + 1) * 128, :])(tb, ps, tok)
            linear_tm(k, cm, uT, KC, lambda kc0, n, g=g: wview(w_in, kc0, n, C_FV + g * 512, 512), 512, epi_fv)
        def epi_ff(tb, ps, tok):
            j = fls.next()
            dve.wait(tok, fls.ds[j].tok())
            td = dve.m(nc.vector.tensor_tensor(out=fx[:, :], in0=ps[:, 0:8], in1=bfg[:, :], op=ALU.add))
            act.wait(td)
            t1 = act.m(nc.scalar.activation(out=fls.bufs[j][:, :], in_=fx[:, :], func=AF.Exp, scale=-1.0))
            act.sw(t1)
            t2 = act.m(nc.scalar.activation(out=fl2[:, :], in_=fls.bufs[j][:, :], func=AF.Ln, bias=one_t[:, 0:1]))
            act.sw(t2)
            ta = act.m(nc.scalar.mul(out=fls.bufs[j][:, :], in_=fl2[:, :], mul=-1.0))
            dve.wait(ta)
            sp.wait(ta)
            fls.ds[j].dma(sp, O["FL"][t * TT + tb * 128:t * TT + (tb + 1) * 128, :], fls.bufs[j][:, :])
            return td
        linear_tm(k, cm, uT, KC, lambda kc0, n: wview(w_in, kc0, n, C_FF, 8), 8, epi_ff)

        def latent_norm(c_off, nch, gains):
            nonlocal cT_free, cn_free
            def epi_c(n, ps, tok):
                eng = act if n % 2 == 0 else dve
                eng.wait(tok)
                if n < 2:
                    eng.wait(cT_free)
                if eng is act:
                    return act.m(nc.scalar.copy(out=cT[:, n, :], in_=ps[:, :]))
                return dve.m(nc.vector.tensor_copy(out=cT[:, n, :], in_=ps[:, :]))
            linear_fm(k, cm, src_u, KC, w_in, c_off, nch * 128, epi_c)
            tc_a, tc_d = act.last_tok, dve.last_tok
            sq = norm_u.sq
            PSS = 7
            pe.wait(cm.ps_free[PSS])
            for n in range(nch):
                js = sq.next()
                act.wait(tc_a, tc_d, sq.free[js])
                ta = act.m(nc.scalar.activation(out=sq.bufs[js][:, :], in_=cT[:, n, :], func=AF.Square))
                pe.wait(ta)
                mm = nc.tensor.matmul(cm.ps[PSS][:, :], lhsT=cm.ones_f[:, :], rhs=sq.bufs[js][:, :],
                                      start=(n == 0), stop=(n == nch - 1))
                sq.free[js] = pe.m(mm)
            rstd = norm_u.rstd
            act.wait(pe.last_tok, norm_u.rstd_free)
            ta = act.m(nc.scalar.activation(out=rstd[:, :], in_=cm.ps[PSS][:, :], func=AF.Sqrt,
                                            scale=1.0 / (nch * 128), bias=norm_u.eps_t[:, 0:1]))
            cm.ps_free[PSS] = ta
            dve.wait(ta, tc_a, cn_free)
            dve(nc.vector.reciprocal(out=rstd[:, :], in_=rstd[:, :]))
            for n in range(nch):
                last = dve.m(nc.vector.scalar_tensor_tensor(out=cn[:, n, :], in0=cT[:, n, :], scalar=gains[:, n:n + 1],
                                                            in1=rstd[:, :], op0=ALU.mult, op1=ALU.mult))
            norm_u.rstd_free = last
            cT_free = last
            return last

        tcn = latent_norm(C_CQ, 8, qng)
        pe.wait(tcn)
        w_uq = W["w_uq"]
        for hg in range(2):
            pieces = [(lambda tl_: slab_view(tl_, 8, 768), wview(w_uq, 0, 8, hg * 768, 768))]
            for two in range(2):
                for hl in range(4):
                    pieces.append((lambda tl_, two=two, hl=hl: tl_[:, 6144 + two * 1024:6144 + (two + 1) * 1024].rearrange(
                        "p (c m) -> p c m", m=128)[:, :, hl * 32:(hl + 1) * 32],
                        wview(w_uq, 0, 8, hg * 768 + hl * 192 + 128 + two * 32, 32)))
            j, tl, tk = cm.wload(pieces)
            v4 = tl[:, 0:8 * 768].rearrange("p (c h f) -> p c h f", h=4, f=192)
            vpe = [tl[:, 6144 + two * 1024:6144 + (two + 1) * 1024].rearrange("p (c m) -> p c m", m=128) for two in range(2)]
            pe.wait(tk)
            for hl in range(4):
                b = (0, 1, 2, 3)[_bank_rot[0] % 4]
                _bank_rot[0] += 1
                pe.wait(cm.ps_free[b])
                for kc in range(8):
                    mm = nc.tensor.matmul(cm.ps[b][:, :], lhsT=v4[:, kc, hl, 0:128], rhs=cn[:, kc, :],
                                          start=(kc == 0), stop=(kc == 7))
                tp = pe.m(mm)
                cm.ps_free[b] = ob.copy_epi(lambda n, hg=hg, hl=hl: O["MQN"][hg, hl, :, cols])(0, cm.ps[b], tp)
            bA, bB = rope_banks[rb % 2]
            rb += 1
            pe.wait(cm.ps_free[bA], cm.ps_free[bB])
            for kc in range(8):
                mm = nc.tensor.matmul(cm.ps[bA][:, :], lhsT=vpe[0][:, kc, :], rhs=cn[:, kc, :],
                                      start=(kc == 0), stop=(kc == 7))
            tA = pe.m(mm)
            for kc in range(8):
                mm = nc.tensor.matmul(cm.ps[bB][:, :], lhsT=vpe[1][:, kc, :], rhs=cn[:, kc, :],
                                      start=(kc == 0), stop=(kc == 7))
            tB = pe.m(mm)
            cm.wfree(j, tB)

            def stores_q(b1, b2, hg=hg):
                p1 = [(O["MQP"][hg, hl, 0:32, cols], b1[hl * 32:(hl + 1) * 32, :]) for hl in range(4)]
                p2 = [(O["MQP"][hg, hl, 32:64, cols], b2[hl * 32:(hl + 1) * 32, :]) for hl in range(4)]
                return p1, p2
            te = rope_epi(cm.ps[bA], cm.ps[bB], tA, tB, cosM, sinM, 128, stores_q)
            cm.ps_free[bA] = te
            cm.ps_free[bB] = te
        cn_free = pe.last_tok
        tcn = latent_norm(C_CKV, 4, kvg)
        pe.wait(tcn)
        w_ukv = W["w_ukv"]
        j, tl, tk = cm.wload([(lambda tl_: slab_view(tl_, 4, 2048), wview(w_ukv, 0, 4, 0, 2048))])
        v4 = tl[:, 0:8192].rearrange("p (c h f) -> p c h f", h=8, f=256)
        pe.wait(tk)
        for h in range(8):
            b = (0, 1, 2, 3)[_bank_rot[0] % 4]
            _bank_rot[0] += 1
            pe.wait(cm.ps_free[b])
            for kc in range(4):
                mm = nc.tensor.matmul(cm.ps[b][:, :], lhsT=v4[:, kc, h, 0:128], rhs=cn[:, kc, :],
                                      start=(kc == 0), stop=(kc == 3))
            tp = pe.m(mm)
            cm.ps_free[b] = ob.copy_epi(lambda n, h=h: O["MKN"][h // 4, h % 4, :, cols])(0, cm.ps[b], tp)
        for hg in range(2):
            for tb in range(4):
                pe.wait(cm.ps_free[tb])
                for hl in range(4):
                    for kc in range(4):
                        mm = nc.tensor.matmul(cm.ps[tb][:, hl * 128:(hl + 1) * 128], lhsT=cn[:, kc, tb * 128:(tb + 1) * 128],
                                              rhs=v4[:, kc, hg * 4 + hl, 128:256], start=(kc == 0), stop=(kc == 3))
                tp = pe.m(mm)
                cm.ps_free[tb] = ob.copy_epi(
                    lambda n, hg=hg, tb=tb: O["MV"][hg, t * TT + tb * 128:t * TT + (tb + 1) * 128, :])(0, cm.ps[tb], tp)
        cm.wfree(j, pe.last_tok)
        cn_free = pe.last_tok
        j, tl, tk = cm.wload([(lambda tl_: slab_view(tl_, KC, 64), wview(w_in, 0, KC, C_KR, 64))])
        v = slab_view(tl, KC, 64)
        bA, bB = rope_banks[rb % 2]
        rb += 1
        pe.wait(tk, cm.ps_free[bA], cm.ps_free[bB])
        for kc in range(KC):
            mm = nc.tensor.matmul(cm.ps[bA][0:32, :], lhsT=v[:, kc, 0:32], rhs=uT[:, kc, :],
                                  start=(kc == 0), stop=(kc == KC - 1))
        tA = pe.m(mm)
        for kc in range(KC):
            mm = nc.tensor.matmul(cm.ps[bB][0:32, :], lhsT=v[:, kc, 32:64], rhs=uT[:, kc, :],
                                  start=(kc == 0), stop=(kc == KC - 1))
        tB = pe.m(mm)
        cm.wfree(j, tB)
        te = rope_epi(cm.ps[bA], cm.ps[bB], tA, tB, cosM, sinM, 32,
                      lambda b1, b2: ([(O["MKP"][0:32, cols], b1[0:32, :])], [(O["MKP"][32:64, cols], b2[0:32, :])]))
        cm.ps_free[bA] = te
        cm.ps_free[bB] = te
        tab_free = te
        norm_u.u_free = pe.last_tok
    k.end_phase()


NB = S // 128
NQT = S // TT


def attn_core(k, cm, name, Qd, Kd, Vd, Od, scale, mask_sb, bias_tab, pe_parts):
    nc = k.nc
    pe, act, dve, sp = k.pe, k.act, k.dve, k.sp
    LAG = 3
    SB = (0, 1, 2, 3)
    OB = ((4, 5), (6, 7))
    qs = [Ring(k, f"{name}q{i}", 2, [p_, S], BF16) for i, (_, p_) in enumerate(Qd)]
    ks = [Ring(k, f"{name}k{i}", 2, [p_, S], BF16) for i, (_, p_) in enumerate(Kd)]
    vs = Ring(k, name + "v", 2, [128, NB, 128], BF16)
    pr = Ring(k, name + "p", 6, [128, TT], BF16, dma=False)
    rec = k.sbuf(name + "rec", [128, TT], F32)
    ob = OutStage(k, name + "ob", 2, [128, TT], BF16)
    head_done = [None, None]
    si = [0]
    for hl in range(4):
        jb = hl % 2
        sp.wait(head_done[jb])
        for i, (qd, p_) in enumerate(Qd):
            qs[i].ds[jb].dma(sp, qs[i].bufs[jb][:, :], qd[hl, :, :])
        tq = [Tok(qs[i].ds[jb].sem, qs[i].ds[jb].cnt) for i in range(len(Qd))]
        tk_ = []
        for i, (kd, p_) in enumerate(Kd):
            src = kd[hl, :, :] if len(kd.shape) == 3 else kd[:, :]
            tk_.append(ks[i].ds[jb].dma(sp, ks[i].bufs[jb][:, :], src))
        tv = vs.ds[jb].dma(sp, vs.bufs[jb][:, :, :],
                           Vd[:, hl * 128:(hl + 1) * 128].rearrange("(j p) d -> p j d", p=128))
        pe.wait(tq, tk_, tv)
        items = [(t, j) for t in range(NQT) for j in range(4 * t + 4)]

        def stage_a(t, j):
            bs = SB[si[0] % len(SB)]
            si[0] += 1
            pe.wait(cm.ps_free[bs])
            for i in range(len(Qd)):
                mm = nc.tensor.matmul(cm.ps[bs][:, :], lhsT=ks[i].bufs[jb][:, j * 128:(j + 1) * 128],
                                      rhs=qs[i].bufs[jb][:, t * TT:(t + 1) * TT],
                                      start=(i == 0), stop=(i == len(Qd) - 1))
            ts = pe.m(mm)
            jj = j - 4 * t
            c0 = max(0, jj) * 128
            jp = pr.next()
            pb = pr.bufs[jp]
            act.wait(ts, pr.free[jp])
            if bias_tab is None:
                ta = act.m(nc.scalar.activation(out=pb[:, c0:TT], in_=cm.ps[bs][:, c0:TT], func=AF.Exp, scale=scale))
            else:
                for qb in range(max(0, jj), 4):
                    i_ = 4 * t + qb
                    ta = act.m(nc.scalar.activation(out=pb[:, qb * 128:(qb + 1) * 128],
                                                    in_=cm.ps[bs][:, qb * 128:(qb + 1) * 128], func=AF.Exp,
                                                    scale=scale, bias=bias_tab[:, i_, j, hl:hl + 1]))
            cm.ps_free[bs] = ta
            tp_ = ta
            if jj >= 0:
                dve.wait(ta)
                tp_ = dve.m(nc.vector.tensor_tensor(out=pb[:, c0:c0 + 128], in0=pb[:, c0:c0 + 128],
                                                    in1=mask_sb[:, :], op=ALU.mult))
            return jp, tp_, c0

        def stage_b(t, j, jp, tp_, c0):
            bo, bd = OB[t % 2]
            nj = 4 * t + 4
            pb = pr.bufs[jp]
            if j == 0:
                pe.wait(cm.ps_free[bo], cm.ps_free[bd])
            pe.wait(tp_)
            nc.tensor.matmul(cm.ps[bo][:, c0:TT], lhsT=vs.bufs[jb][:, j, :], rhs=pb[:, c0:TT],
                             start=(j == 0), stop=(j == nj - 1))
            mm = nc.tensor.matmul(cm.ps[bd][:, c0:TT], lhsT=cm.ones_b[:, :], rhs=pb[:, c0:TT],
                                  start=(j == 0), stop=(j == nj - 1))
            pr.free[jp] = pe.m(mm)
            if j == nj - 1:
                tacc = pe.last_tok
                dve.wait(tacc)
                dve(nc.vector.reciprocal(out=rec[:, :], in_=cm.ps[bd][:, :]))
                jo, obuf = ob.slot(dve)
                to = dve.m(nc.vector.tensor_tensor(out=obuf[:, :], in0=cm.ps[bo][:, :], in1=rec[:, :], op=ALU.mult))
                cm.ps_free[bo] = to
                cm.ps_free[bd] = to
                ob.store(jo, to, [(Od[hl * 128:(hl + 1) * 128, t * TT:(t + 1) * TT], obuf[:, :])])

        pend = []
        for idx in range(len(items) + LAG):
            if idx < len(items):
                t, j = items[idx]
                pend.append((t, j) + stage_a(t, j))
            if idx >= LAG:
                stage_b(*pend.pop(0))
        head_done[jb] = pe.last_tok


def phase_fox(k, cm, I, cst, Od):
    nc = k.nc
    pe, act, dve, sp = k.pe, k.act, k.dve, k.sp
    k.begin_phase()
    tri = k.sbuf("fx_tri", [128, 128], BF16)
    U = k.sbuf("fx_U", [128, 128], F32)
    M = k.sbuf("fx_M", [128, 128], F32)
    lf = k.sbuf("fx_lf", [128, NB, 4], F32)
    totT = k.sbuf("fx_totT", [128, 128], F32)
    F_sb = k.sbuf("fx_F", [128, NB, 4], F32)
    fc = k.sbuf("fx_fc", [128, NB, 4], F32)
    Ball = k.sbuf("fx_B", [128, NB, NB, 4], F32)
    ds = DSem(k, "fx_c")
    ds.dma(sp, tri[:, :], cst["tri"][:, :])
    ds.dma(sp, U[:, :], cst["U"][:, :])
    ds.dma(sp, M[:, :], cst["M"][:, :])
    with nc.allow_non_contiguous_dma(reason="tiny forget-gate table"):
        t0 = ds.dma(sp, lf[:, :, :], I["FL"].rearrange("(j p) h -> p j h", p=128))
    lf2 = lf[:, :, :].rearrange("p j h -> p (j h)")
    pe.wait(t0, cm.ps_free[0], cm.ps_free[1], cm.ps_free[2])
    t1 = pe.m(nc.tensor.matmul(cm.ps[0][:, 0:128], lhsT=lf2, rhs=cm.ones_f[:, :], start=True, stop=True))
    dve.wait(t1)
    t2 = dve.m(nc.vector.tensor_copy(out=totT[:, :], in_=cm.ps[0][:, 0:128]))
    pe.wait(t2)
    t3 = pe.m(nc.tensor.matmul(cm.ps[1][:, 0:128], lhsT=totT[:, :], rhs=M[:, :], start=True, stop=True))
    nc.tensor.matmul(cm.ps[2][:, 0:128], lhsT=U[:, :], rhs=lf2, start=True, stop=False)
    t4 = pe.m(nc.tensor.matmul(cm.ps[2][:, 0:128], lhsT=totT[:, :], rhs=M[:, :], start=False, stop=True))
    dve.wait(t3, t4)
    dve(nc.vector.tensor_copy(out=F_sb[:, :, :].rearrange("p j h -> p (j h)"), in_=cm.ps[1][:, 0:128]))
    t5 = dve.m(nc.vector.tensor_copy(out=fc[:, :, :].rearrange("p j h -> p (j h)"), in_=cm.ps[2][:, 0:128]))
    for b in range(3):
        cm.ps_free[b] = t5
    dve.sw(t5)
    last = None
    for i in range(NB):
        last = dve.m(nc.vector.tensor_tensor(out=Ball[:, i, 0:i + 1, :],
                                             in0=F_sb[:, i:i + 1, :].to_broadcast([128, i + 1, 4]),
                                             in1=fc[:, 0:i + 1, :], op=ALU.subtract))
    act.wait(last)
    attn_core(k, cm, "fx", [(I["FQ"], 128)], [(I["FK"], 128)], I["FV"], Od, 128.0 ** -0.5, tri, Ball, None)
    k.end_phase()


def phase_mla(k, cm, I, cst, Od):
    nc = k.nc
    k.begin_phase()
    cmask = k.sbuf("ml_cm", [128, 128], BF16)
    ds = DSem(k, "ml_c")
    t0 = ds.dma(k.sp, cmask[:, :], cst["cmask"][:, :])
    k.dve.wait(t0)
    attn_core(k, cm, "ml", [(I["MQN"], 128), (I["MQP"], 64)], [(I["MKN"], 128), (I["MKP"], 64)], I["MV"], Od,
              192.0 ** -0.5, cmask, None, None)
    k.end_phase()


def phase_ret(k, cm, I, cst, Oraw, Od):
    nc = k.nc
    pe, act, dve, sp = k.pe, k.act, k.dve, k.sp
    k.begin_phase()
    NCH = S // 64
    DmT = k.sbuf("rt_Dm", [128, 4, 64], F32)
    xi = k.sbuf("rt_xi", [128, 4, 64], F32)
    zeta = k.sbuf("rt_zeta", [128, 4], F32)
    gC = k.sbuf("rt_gC", [128, 4], F32)
    ds = DSem(k, "rt_c")
    ds.dma(sp, DmT[:, :, :], cst["DmT"][:, :, :])
    ds.dma(sp, xi[:, :, :], cst["xi"][:, :, :])
    ds.dma(sp, zeta[:, :], cst["zeta"][:, :])
    t0 = ds.dma(sp, gC[:, :], cst["gC"][:, :])
    dve.wait(t0)
    q_sb = [k.sbuf(f"rt_q{i}", [128, S], BF16) for i in range(2)]
    qx_sb = [k.sbuf(f"rt_qx{i}", [128, S], BF16) for i in range(2)]
    k_sb = [k.sbuf(f"rt_k{i}", [128, S], BF16) for i in range(2)]
    v_sb = [k.sbuf(f"rt_v{i}", [128, NB, 256], BF16) for i in range(2)]
    kz_sb = [k.sbuf(f"rt_kz{i}", [128, NB, 128], BF16) for i in range(2)]
    st_f = [k.sbuf(f"rt_sf{i}", [128, 256], F32) for i in range(2)]
    st_b = [Ring(k, f"rt_sb{i}", 2, [128, 256], BF16, dma=False) for i in range(2)]
    at_r = Ring(k, "rt_at", 4, [128, 64], BF16, dma=False)
    lds = [DSem(k, f"rt_l{i}") for i in range(2)]
    oraw = OutStage(k, "rt_or", 4, [128, TT], F32)
    grp_done = None
    for grp in range(2):
        heads = [grp * 2, grp * 2 + 1]
        sp.wait(grp_done)
        tl = []
        for i, hl in enumerate(heads):
            lds[i].dma(sp, q_sb[i][:, :], I["RQ"][hl, :, :])
            lds[i].dma(sp, k_sb[i][:, :], I["RK"][hl, :, :])
            tl.append(lds[i].dma(sp, v_sb[i][:, :, :],
                                 I["RV"][:, hl * 256:(hl + 1) * 256].rearrange("(j p) d -> p j d", p=128)))
        for i, hl in enumerate(heads):
            dve.wait(tl[i])
            for t in range(NQT):
                dve(nc.vector.tensor_tensor(
                    out=qx_sb[i][:, t * TT:(t + 1) * TT].rearrange("p (c f) -> p c f", f=64),
                    in0=q_sb[i][:, t * TT:(t + 1) * TT].rearrange("p (c f) -> p c f", f=64),
                    in1=xi[:, hl:hl + 1, :].to_broadcast([128, 8, 64]), op=ALU.mult))
            dve(nc.vector.memset(st_f[i][:, :], 0.0))
            dve(nc.vector.memset(st_b[i].bufs[0][:, :], 0.0))
            st_b[i].i = 0
            pe.wait(tl[i])
            for j in range(NB):
                b = j % 2
                pe.wait(cm.ps_free[b])
                tp = pe.m(nc.tensor.matmul(cm.ps[b][:, 0:128], lhsT=k_sb[i][:, j * 128:(j + 1) * 128],
                                           rhs=cm.ident_b[:, :], start=True, stop=True))
                dve.wait(tp)
                cm.ps_free[b] = dve.m(nc.vector.tensor_scalar(out=kz_sb[i][:, j, :], in0=cm.ps[b][:, 0:128],
                                                             scalar1=zeta[:, hl:hl + 1], scalar2=None, op0=ALU.mult))
        tprep = dve.tail()
        pe.wait(tprep)
        act.wait(tprep)
        st_tok = [tprep, tprep]
        stf_tok = [None, None]
        for n in range(NCH):
            par = n % 2
            pS, pO, pD = (0, 1, 2) if par == 0 else (3, 4, 5)
            j = n // 2
            hp = (n % 2) * 64
            P = slice(hp, hp + 64)
            cs = slice(n * 64, (n + 1) * 64)
            pe.wait(cm.ps_free[pS], cm.ps_free[pO], cm.ps_free[pD])
            tS = []
            for i in range(2):
                tS.append(pe.m(nc.tensor.matmul(cm.ps[pS][P, i * 64:(i + 1) * 64], lhsT=k_sb[i][:, cs], rhs=q_sb[i][:, cs],
                                                start=True, stop=True)))
            ats = []
            for i, hl in enumerate(heads):
                ja = at_r.next()
                dve.wait(tS[i], at_r.free[ja])
                ta = dve.m(nc.vector.tensor_tensor(out=at_r.bufs[ja][P, :], in0=cm.ps[pS][P, i * 64:(i + 1) * 64],
                                                   in1=DmT[P, hl, :], op=ALU.mult))
                ats.append((ja, ta))
            cm.ps_free[pS] = ats[-1][1]
            tO = []
            for i in range(2):
                ja, ta = ats[i]
                cur = st_b[i].bufs[st_b[i].i % 2]
                pe.wait(ta, st_tok[i])
                for c in range(2):
                    oc = (i * 2 + c) * 64
                    nc.tensor.matmul(cm.ps[pO][:, oc:oc + 64], lhsT=v_sb[i][P, j, c * 128:(c + 1) * 128],
                                     rhs=at_r.bufs[ja][P, :], start=True, stop=False)
                    mm = nc.tensor.matmul(cm.ps[pO][:, oc:oc + 64], lhsT=cur[:, c * 128:(c + 1) * 128],
                                          rhs=qx_sb[i][:, cs], start=False, stop=True)
                tO.append(pe.m(mm))
                at_r.free[ja] = tO[-1]
            tD = []
            for i in range(2):
                tD.append(pe.m(nc.tensor.matmul(cm.ps[pD][:, i * 256:(i + 1) * 256], lhsT=kz_sb[i][P, j, :],
                                                rhs=v_sb[i][P, j, :], start=True, stop=True)))
            for i, hl in enumerate(heads):
                dve.wait(tD[i], stf_tok[i])
                tf = dve.m(nc.vector.scalar_tensor_tensor(out=st_f[i][:, :], in0=st_f[i][:, :], scalar=gC[:, hl:hl + 1],
                                                          in1=cm.ps[pD][:, i * 256:(i + 1) * 256],
                                                          op0=ALU.mult, op1=ALU.add))
                st_b[i].i += 1
                nxt = st_b[i].bufs[st_b[i].i % 2]
                act.wait(tf, tO[i])
                ta = act.m(nc.scalar.copy(out=nxt[:, :], in_=st_f[i][:, :]))
                st_tok[i] = ta
                stf_tok[i] = ta
            cm.ps_free[pD] = dve.last_tok
            if n % 8 == 0:
                stg = [oraw.slot(act) for _ in range(4)]
            act.wait(tO[1])
            for i in range(2):
                for c in range(2):
                    oc = (i * 2 + c) * 64
                    ta = act.m(nc.scalar.copy(out=stg[i * 2 + c][1][:, (n % 8) * 64:(n % 8 + 1) * 64],
                                              in_=cm.ps[pO][:, oc:oc + 64]))
            cm.ps_free[pO] = ta
            if n % 8 == 7:
                tt_ = n // 8
                for i, hl in enumerate(heads):
                    for c in range(2):
                        jj_, buf = stg[i * 2 + c]
                        oraw.store(jj_, ta, [(Oraw[hl * 256 + c * 128:hl * 256 + (c + 1) * 128, tt_ * TT:(tt_ + 1) * TT], buf[:, :])])
        grp_done = pe.last_tok
    k.end_phase()
    k.begin_phase()
    setup_norm(k)
    rg = k.sbuf("rt_g", [128, 8], F32)
    ds2 = DSem(k, "rt_g")
    dve.wait(ds2.dma(sp, rg[:, :], cst["ret_g"][:, :]))
    oin = Ring(k, "rt_oin", 4, [128, TT], F32)
    gin = Ring(k, "rt_gin", 4, [128, TT], BF16)
    ytmp = k.sbuf("rt_y", [128, TT], F32)
    ob = OutStage(k, "rt_ob", 3, [128, TT], BF16)
    sq = norm_u.sq
    rstd = norm_u.rstd
    PSS = 7
    for hl in range(4):
        for t in range(NQT):
            cols = slice(t * TT, (t + 1) * TT)
            ld = []
            for c in range(2):
                r0 = hl * 256 + c * 128
                jo = oin.next()
                sp.wait(oin.free[jo])
                to_ = oin.ds[jo].dma(sp, oin.bufs[jo][:, :], Oraw[r0:r0 + 128, cols])
                jg = gin.next()
                sp.wait(gin.free[jg])
                tg_ = gin.ds[jg].dma(sp, gin.bufs[jg][:, :], I["RG"][r0:r0 + 128, cols])
                ld.append((jo, to_, jg, tg_))
            pe.wait(cm.ps_free[PSS])
            for c in range(2):
                jo, to_, jg, tg_ = ld[c]
                js = sq.next()
                act.wait(to_, sq.free[js])
                ta = act.m(nc.scalar.activation(out=sq.bufs[js][:, :], in_=oin.bufs[jo][:, :], func=AF.Square))
                pe.wait(ta)
                sq.free[js] = pe.m(nc.tensor.matmul(cm.ps[PSS][:, :], lhsT=cm.ones_f[:, :], rhs=sq.bufs[js][:, :],
                                                    start=(c == 0), stop=(c == 1)))
            act.wait(pe.last_tok, norm_u.rstd_free)
            ta = act.m(nc.scalar.activation(out=rstd[:, :], in_=cm.ps[PSS][:, :], func=AF.Sqrt, scale=1.0 / 256,
                                            bias=norm_u.eps_t[:, 0:1]))
            cm.ps_free[PSS] = ta
            dve.wait(ta)
            dve(nc.vector.reciprocal(out=rstd[:, :], in_=rstd[:, :]))
            for c in range(2):
                jo, to_, jg, tg_ = ld[c]
                r0 = hl * 256 + c * 128
                dve.wait(to_, tg_)
                dve(nc.vector.scalar_tensor_tensor(out=ytmp[:, :], in0=oin.bufs[jo][:, :], scalar=rg[:, hl * 2 + c:hl * 2 + c + 1],
                                                   in1=rstd[:, :], op0=ALU.mult, op1=ALU.mult))
                jb_, obuf = ob.slot(dve)
                ty = dve.m(nc.vector.tensor_tensor(out=obuf[:, :], in0=ytmp[:, :], in1=gin.bufs[jg][:, :], op=ALU.mult))
                oin.free[jo] = ty
                gin.free[jg] = ty
                ob.store(jb_, ty, [(Od[r0:r0 + 128, cols], obuf[:, :])])
            norm_u.rstd_free = dve.last_tok
    k.end_phase()


def phase_mixc(k, cm, hT, W, Oin):
    nc = k.nc
    pe, act, dve, sp = k.pe, k.act, k.dve, k.sp
    k.begin_phase()
    setup_norm(k)
    g_sb = k.sbuf("mc_g", [128, KC], F32)
    bg = k.sbuf("mc_bg", [128, 3, KC], F32)
    pds = DSem(k, "mc_p")
    pds.dma(sp, g_sb[:, :], W["mix_g"][:, :])
    tparam = pds.dma(sp, bg[:, :, :], W["bg"][:, :, :])
    dve.wait(tparam)
    act.wait(tparam)
    uT = k.sbuf("mc_u", [128, KC, TT], BF16)
    oT = k.sbuf("mc_o", [128, KC, TT], BF16)
    mg = k.sbuf("mc_m", [128, KC, TT], BF16)
    macc = [k.sbuf(f"mc_acc{i}", [128, TT], F32) for i in range(2)]
    mtmp = k.sbuf("mc_tmp", [128, TT], F32)
    stage = Ring(k, "mc_st", 3, [128, TT], F32)
    sa = Ring(k, "mc_sa", 2, [128, TT], F32, dma=False)
    res = Ring(k, "mc_res", 3, [128, TT], F32)
    ods = DSem(k, "mc_o")
    o_free = None
    mg_free = None
    ups = [(W["w_up_ret"], 16, 0), (W["w_up_fox"], 8, 16), (W["w_up_mla"], 8, 24)]
    pi = 0
    for t in range(NT):
        cols = slice(t * TT, (t + 1) * TT)
        tu = norm_u(k, cm, hT, t, g_sb, uT, stage)
        sp.wait(o_free)
        ods.dma(sp, oT[:, 0:16, :], Oin["ORc"][:, cols].rearrange("(c p) t -> p c t", p=128))
        ods.dma(sp, oT[:, 16:24, :], Oin["OFc"][:, cols].rearrange("(c p) t -> p c t", p=128))
        to_ = ods.dma(sp, oT[:, 24:32, :], Oin["OMc"][:, cols].rearrange("(c p) t -> p c t", p=128))
        pe.wait(tu, to_)
        for s in range(D // 256):
            for i in range(3):
                wu, nku, off = ups[i]
                jg, tlg, tkg = cm.wload([(lambda tl_: slab_view(tl_, KC, 256), wview(W["w_gate"][i], 0, KC, s * 256, 256))])
                ju, tlu, tku = cm.wload([(lambda tl_: slab_view(tl_, nku, 256), wview(wu, 0, nku, s * 256, 256))])
                vg = slab_view(tlg, KC, 256)
                vu = slab_view(tlu, nku, 256)
                pe.wait(tkg, tku)
                for cc in range(2):
                    n = s * 2 + cc
                    pa = (pi % 2) * 2
                    pb = pa + 1
                    pi += 1
                    pe.wait(cm.ps_free[pa], cm.ps_free[pb])
                    for kc in range(KC):
                        mm = nc.tensor.matmul(cm.ps[pa][:, :], lhsT=vg[:, kc, cc * 128:(cc + 1) * 128], rhs=uT[:, kc, :],
                                              start=(kc == 0), stop=(kc == KC - 1))
                    tpa = pe.m(mm)
                    for kc in range(nku):
                        mm = nc.tensor.matmul(cm.ps[pb][:, :], lhsT=vu[:, kc, cc * 128:(cc + 1) * 128], rhs=oT[:, off + kc, :],
                                              start=(kc == 0), stop=(kc == nku - 1))
                    tpb = pe.m(mm)
                    js = sa.next()
                    act.wait(tpa, sa.free[js])
                    tsa = act.m(nc.scalar.activation(out=sa.bufs[js][:, :], in_=cm.ps[pa][:, :], func=AF.Sigmoid,
                                                     bias=bg[:, i, n:n + 1]))
                    cm.ps_free[pa] = tsa
                    dve.wait(tsa, tpb)
                    if i == 0:
                        th = dve.m(nc.vector.tensor_tensor(out=macc[cc][:, :], in0=cm.ps[pb][:, :], in1=sa.bufs[js][:, :],
                                                           op=ALU.mult))
                    else:
                        th = dve.m(nc.vector.tensor_tensor(out=mtmp[:, :], in0=cm.ps[pb][:, :], in1=sa.bufs[js][:, :],
                                                           op=ALU.mult))
                        if i == 1:
                            dve(nc.vector.tensor_tensor(out=macc[cc][:, :], in0=macc[cc][:, :], in1=mtmp[:, :], op=ALU.add))
                        else:
                            if n == 0:
                                dve.wait(mg_free)
                            dve(nc.vector.tensor_tensor(out=mg[:, n, :], in0=macc[cc][:, :], in1=mtmp[:, :], op=ALU.add))
                    sa.free[js] = th
                    cm.ps_free[pb] = th
                cm.wfree(jg, tpb)
                cm.wfree(ju, tpb)
        norm_u.u_free = pe.last_tok
        o_free = pe.last_tok
        tmg = dve.tail()
        pe.wait(tmg)
        def epi_out(n, ps, tok):
            j = stage.next()
            sp.wait(stage.free[j])
            tk = stage.ds[j].dma(sp, stage.bufs[j][:, :], hT[n * 128:(n + 1) * 128, cols])
            jr = res.next()
            dve.wait(tk, tok, res.ds[jr].tok())
            tr = dve.m(nc.vector.tensor_tensor(out=res.bufs[jr][:, :], in0=ps[:, :], in1=stage.bufs[j][:, :], op=ALU.add))
            stage.free[j] = tr
            sp.wait(tr)
            res.ds[jr].dma(sp, hT[n * 128:(n + 1) * 128, cols], res.bufs[jr][:, :])
            return tr
        linear_fm(k, cm, lambda kc: mg[:, kc, :], KC, W["w_out"], 0, D, epi_out)
        mg_free = pe.last_tok
    k.end_phase()


def phase_final(k, cm, hT, gain, out):
    nc = k.nc
    pe, act, dve, sp = k.pe, k.act, k.dve, k.sp
    k.begin_phase()
    setup_norm(k)
    g_sb = k.sbuf("fn_g", [128, KC], F32)
    gds = DSem(k, "fn_g")
    dve.wait(gds.dma(sp, g_sb[:, :], gain[:, :]))
    stage = Ring(k, "fn_st", 3, [128, TT], F32)
    yb = Ring(k, "fn_y", 8, [128, TT], F32, dma=False)
    ost = OutStage(k, "fn_o", 4, [128, TT], F32)
    for t in range(NT):
        pend = []

        def cb(c, st_buf, rstd):
            jy = yb.next()
            dve.wait(yb.free[jy])
            ty = dve.m(nc.vector.scalar_tensor_tensor(out=yb.bufs[jy][:, :], in0=st_buf[:, :], scalar=g_sb[:, c:c + 1],
                                                      in1=rstd[:, :], op0=ALU.mult, op1=ALU.mult))
            pe.wait(ty)
            if c % 4 == 0:
                for tb in range(4):
                    pe.wait(cm.ps_free[tb])
            for tb in range(4):
                mm = nc.tensor.matmul(cm.ps[tb][:, (c % 4) * 128:(c % 4 + 1) * 128], lhsT=yb.bufs[jy][:, tb * 128:(tb + 1) * 128],
                                      rhs=cm.ident[:, :], start=True, stop=True)
            tp = pe.m(mm)
            yb.free[jy] = tp
            if c % 4 == 3:
                cg = c // 4
                for tb in range(4):
                    eng = act if tb % 2 == 0 else dve
                    jo, obuf = ost.slot(eng)
                    eng.wait(tp)
                    if eng is act:
                        te = act.m(nc.scalar.copy(out=obuf[:, :], in_=cm.ps[tb][:, :]))
                    else:
                        te = dve.m(nc.vector.tensor_copy(out=obuf[:, :], in_=cm.ps[tb][:, :]))
                    cm.ps_free[tb] = te
                    ost.store(jo, te, [(out[t * TT + tb * 128:t * TT + (tb + 1) * 128, cg * 512:(cg + 1) * 512], obuf[:, :])])
            return ty
        norm_u(k, cm, hT, t, g_sb, None, stage, out_f32=cb)
    k.end_phase()


def phase_copy(k, cm, src, dst):
    ds = DSem(k, "cp")
    n = 8
    rows = src.shape[0] // n
    for i in range(n):
        ds.dma(k.sp, dst[i * rows:(i + 1) * rows, :], src[i * rows:(i + 1) * rows, :])
    k.barrier()


BUNDLE = {
    "RQ": ([2, 4, 128, T], BF16), "RK": ([2, 4, 128, T], BF16), "RV": ([2, T, 1024], BF16),
    "RG": ([2, 1024, T], BF16), "FQ": ([2, 4, 128, T], BF16), "FK": ([2, 4, 128, T], BF16),
    "FV": ([2, T, 512], BF16), "FL": ([T, 8], F32), "MQN": ([2, 4, 128, T], BF16),
    "MQP": ([2, 4, 64, T], BF16), "MKN": ([2, 4, 128, T], BF16), "MV": ([2, T, 512], BF16),
    "MKP": ([64, T], BF16),
}


def decl_ffn(k, pre):
    return (k.dt(pre + "_g", [128, KC], F32, "ExternalInput"),
            k.dt(pre + "_w13", [D, 2 * DFF], F32, "ExternalInput"),
            k.dt(pre + "_w2", [DFF, D], F32, "ExternalInput"))


def decl_mixa(k):
    W = {"mix_g": k.dt("mix_g", [128, KC], F32, "ExternalInput"),
         "w_in": k.dt("w_in", [D, INW], F32, "ExternalInput"),
         "bfg": k.dt("bfg", [128, 8], F32, "ExternalInput"),
         "qn_g": k.dt("qn_g", [128, 8], F32, "ExternalInput"),
         "kvn_g": k.dt("kvn_g", [128, 4], F32, "ExternalInput"),
         "w_uq": k.dt("w_uq", [1024, 1536], F32, "ExternalInput"),
         "w_ukv": k.dt("w_ukv", [512, 2048], F32, "ExternalInput")}
    pos = k.dt("pos", [1, T], I32, "ExternalInput")
    inv = k.dt("c_inv", [128, 2], F32, "ExternalInput")
    tabs = {n: k.dt("tab_" + n, [128, T], F32, "Internal") for n in ("cosR", "sinR", "cosM", "sinM")}
    O = {n: k.dt("o_" + n, sh, dt_, "ExternalOutput") for n, (sh, dt_) in BUNDLE.items()}
    return W, pos, inv, tabs, O


MIXB_IN = {
    "RQ": ([4, 128, S], BF16), "RK": ([4, 128, S], BF16), "RV": ([S, 1024], BF16), "RG": ([1024, S], BF16),
    "FQ": ([4, 128, S], BF16), "FK": ([4, 128, S], BF16), "FV": ([S, 512], BF16), "FL": ([S, 4], F32),
    "MQN": ([4, 128, S], BF16), "MQP": ([4, 64, S], BF16), "MKN": ([4, 128, S], BF16), "MV": ([S, 512], BF16),
    "MKP": ([64, S], BF16),
}
MIXB_CST = {
    "tri": ([128, 128], BF16), "cmask": ([128, 128], BF16), "U": ([128, 128], F32), "M": ([128, 128], F32),
    "DmT": ([128, 4, 64], F32), "xi": ([128, 4, 64], F32), "zeta": ([128, 4], F32), "gC": ([128, 4], F32),
    "ret_g": ([128, 8], F32),
}


def mixb_consts(half, ret_norm_l):
    bf = ml_dtypes.bfloat16
    p = np.arange(128)
    tri = (p[:, None] <= p[None, :]).astype(np.float32)
    cmask = ((p[:, None] // 64) <= (p[None, :] // 64)).astype(np.float32)
    jh = np.arange(128)
    Mm = ((jh[:, None] % 4 == jh[None, :] % 4) & (jh[:, None] // 4 < jh[None, :] // 4)).astype(np.float32)
    heads = half * 4 + np.arange(4)
    log_g = np.log1p(-np.exp2(-5.0 - heads.astype(np.float64)))
    idx = np.arange(64, dtype=np.float64)
    dm = np.exp(log_g[:, None, None] * np.abs(idx[:, None] - idx[None, :])) * (128.0 ** -0.5)
    DmT = np.tile(dm.transpose(1, 0, 2), (2, 1, 1))
    xi = np.exp(log_g[:, None] * (idx + 1.0)) * (128.0 ** -0.5)
    xi = np.tile(xi[None], (128, 1, 1))
    zeta = np.exp(log_g[None, :] * (63.0 - (p % 64))[:, None])
    gC = np.tile(np.exp(log_g * 64.0)[None, :], (128, 1))
    rg = np.asarray(ret_norm_l).reshape(8, 2, 128)[half * 4:half * 4 + 4]
    ret_g = np.ascontiguousarray(rg.transpose(2, 0, 1).reshape(128, 8))
    return {"tri": tri.astype(bf), "cmask": cmask.astype(bf), "U": tri, "M": Mm,
            "DmT": DmT.astype(np.float32), "xi": xi.astype(np.float32), "zeta": zeta.astype(np.float32),
            "gC": gC.astype(np.float32), "ret_g": ret_g.astype(np.float32)}


def build_mixb(parts=("ret", "fox", "mla")):
    k = K()
    cst0 = {"ident": k.dt("c_ident", [128, 128], F32, "ExternalInput")}
    cm = Common(k, cst0)
    I = {n: k.dt("b_" + n, sh, dt_, "ExternalInput") for n, (sh, dt_) in MIXB_IN.items()}
    cst = {n: k.dt("cb_" + n, sh, dt_, "ExternalInput") for n, (sh, dt_) in MIXB_CST.items()}
    OR = k.dt("o_OR", [1024, S], BF16, "ExternalOutput")
    OF = k.dt("o_OF", [512, S], BF16, "ExternalOutput")
    OM = k.dt("o_OM", [512, S], BF16, "ExternalOutput")
    Oraw = k.dt("oraw", [1024, S], F32, "Internal")
    if "fox" in parts:
        phase_fox(k, cm, I, cst, OF)
    if "mla" in parts:
        phase_mla(k, cm, I, cst, OM)
    if "ret" in parts:
        phase_ret(k, cm, I, cst, Oraw, OR)
    k.barrier()
    return k


def decl_mixc(k):
    W = {"mix_g": k.dram.get("mix_g") if "mix_g" in k.dram else k.dt("mix_g", [128, KC], F32, "ExternalInput"),
         "bg": k.dt("bg", [128, 3, KC], F32, "ExternalInput"),
         "w_gate": k.dt("w_gate", [3, D, D], F32, "ExternalInput"),
         "w_up_ret": k.dt("w_up_ret", [2048, D], F32, "ExternalInput"),
         "w_up_fox": k.dt("w_up_fox", [1024, D], F32, "ExternalInput"),
         "w_up_mla": k.dt("w_up_mla", [1024, D], F32, "ExternalInput"),
         "w_out": k.dt("w_out", [D, D], F32, "ExternalInput")}
    Oin = {"ORc": k.dt("ORc", [2048, T], BF16, "ExternalInput"),
           "OFc": k.dt("OFc", [1024, T], BF16, "ExternalInput"),
           "OMc": k.dt("OMc", [1024, T], BF16, "ExternalInput")}
    return W, Oin


def build(kind):
    k = K()
    cst = {"ident": k.dt("c_ident", [128, 128], F32, "ExternalInput")}
    cm = Common(k, cst)
    hT = k.dt("hT", [D, T], F32, "ExternalOutput")
    if kind == "TEST_FFN":
        x = k.dt("x", [T, D], F32, "ExternalInput")
        phase_p0(k, cm, x, hT)
        g, w13, w2 = decl_ffn(k, "ffn1")
        phase_ffn(k, cm, hT, g, w13, w2)
        k.barrier()
        return k
    if kind == "A0":
        x = k.dt("x", [T, D], F32, "ExternalInput")
        phase_p0(k, cm, x, hT)
    else:
        hin = k.dt("hT_in", [D, T], F32, "ExternalInput")
        phase_copy(k, cm, hin, hT)
    if kind in ("CM", "C1"):
        W, Oin = decl_mixc(k)
        phase_mixc(k, cm, hT, W, Oin)
        g, w13, w2 = decl_ffn(k, "ffn2")
        phase_ffn(k, cm, hT, g, w13, w2)
    if kind in ("A0", "A1"):
        g, w13, w2 = decl_ffn(k, "ffn1")
        phase_ffn(k, cm, hT, g, w13, w2)
        W = {"mix_g": k.dt("mix_g", [128, KC], F32, "ExternalInput"),
             "w_in": k.dt("w_in", [D, INW], F32, "ExternalInput"),
             "bfg": k.dt("bfg", [128, 8], F32, "ExternalInput"),
             "qn_g": k.dt("qn_g", [128, 8], F32, "ExternalInput"),
             "kvn_g": k.dt("kvn_g", [128, 4], F32, "ExternalInput"),
             "w_uq": k.dt("w_uq", [1024, 1536], F32, "ExternalInput"),
             "w_ukv": k.dt("w_ukv", [512, 2048], F32, "ExternalInput")}
        pos = k.dt("pos", [1, T], I32, "ExternalInput")
        inv = k.dt("c_inv", [128, 2], F32, "ExternalInput")
        tabs = {n: k.dt("tab_" + n, [128, T], F32, "Internal") for n in ("cosR", "sinR", "cosM", "sinM")}
        O = {n: k.dt("o_" + n, sh, dt_, "ExternalOutput") for n, (sh, dt_) in BUNDLE.items()}
        phase_tables(k, cm, pos, inv, tabs)
        phase_mixa(k, cm, hT, W, tabs, O)
    if kind == "C1":
        fg = k.dt("final_g", [128, KC], F32, "ExternalInput")
        out = k.dt("out", [T, D], F32, "ExternalOutput")
        phase_final(k, cm, hT, fg, out)
    k.barrier()
    return k


def pvec(v, n=128):
    v = np.asarray(v)
    return np.ascontiguousarray(v.reshape(-1, n).T)


NCORES = 8
_CACHE = {}


def _prog(kind):
    if kind not in _CACHE:
        _CACHE[kind] = build_mixb() if kind == "B" else build(kind)
    return _CACHE[kind]


def _rope_inv():
    inv64 = (10000.0 ** (-np.arange(64, dtype=np.float32) / 64)).astype(np.float32)
    inv32 = (10000.0 ** (-np.arange(32, dtype=np.float32) / 32)).astype(np.float32)
    p = np.arange(128)
    return np.ascontiguousarray(np.stack([inv64[p % 64], inv32[p % 32]], axis=1).astype(np.float32))


def _run(kind, in_maps):
    k = _prog(kind)
    res = run_bass_kernel_spmd(k.nc, in_maps, core_ids=list(range(NCORES)))
    return res.results


def _mixa_inputs(inp, l, pre=""):
    return {pre + "mix_g": pvec(inp["mix_norm"][l]), "w_in": np.asarray(inp["w_in"][l]),
            "bfg": np.ascontiguousarray(np.broadcast_to(np.asarray(inp["b_forget"][l])[None, :], (128, 8))),
            "qn_g": pvec(inp["mla_q_norm"][l]), "kvn_g": pvec(inp["mla_kv_norm"][l]),
            "w_uq": np.asarray(inp["w_uq"][l]), "w_ukv": np.asarray(inp["w_ukv"][l]), "c_inv": _rope_inv()}


def _ffn_inputs(inp, l, which):
    return {which + "_g": pvec(inp[which + "_norm"][l]), which + "_w13": np.asarray(inp[which + "_w13"][l]),
            which + "_w2": np.asarray(inp[which + "_w2"][l])}


def _mixc_inputs(inp, l):
    bgt = np.asarray(inp["b_gate"][l]).reshape(3, KC, 128).transpose(2, 0, 1)
    return {"mix_g": pvec(inp["mix_norm"][l]), "bg": np.ascontiguousarray(bgt), "w_gate": np.asarray(inp["w_gate"][l]),
            "w_up_ret": np.asarray(inp["w_up_ret"][l]), "w_up_fox": np.asarray(inp["w_up_fox"][l]),
            "w_up_mla": np.asarray(inp["w_up_mla"][l]), "w_out": np.asarray(inp["w_out"][l])}


def _regroup_b(resA, inp, l):
    maps = []
    ident = np.eye(128, dtype=np.float32)
    for c in range(NCORES):
        b, half = c // 2, c % 2
        r0, r1 = resA[2 * b], resA[2 * b + 1]
        m = {"c_ident": ident}
        for n in ("RQ", "RK", "FQ", "FK", "MQN", "MQP", "MKN"):
            m["b_" + n] = np.concatenate([r0["o_" + n][half], r1["o_" + n][half]], axis=-1)
        for n in ("RV", "FV", "MV"):
            m["b_" + n] = np.concatenate([r0["o_" + n][half], r1["o_" + n][half]], axis=0)
        m["b_RG"] = np.concatenate([r0["o_RG"][half], r1["o_RG"][half]], axis=-1)
        m["b_MKP"] = np.concatenate([r0["o_MKP"], r1["o_MKP"]], axis=-1)
        fl = np.concatenate([r0["o_FL"], r1["o_FL"]], axis=0)
        m["b_FL"] = np.ascontiguousarray(fl[:, half * 4:half * 4 + 4])
        for n, v in mixb_consts(half, inp["ret_norm"][l]).items():
            m["cb_" + n] = v
        maps.append(m)
    return maps


def _regroup_c(resB):
    outs = []
    for c in range(NCORES):
        b, half = c // 2, c % 2
        r0, r1 = resB[2 * b], resB[2 * b + 1]
        sl = slice(half * T, (half + 1) * T)
        outs.append({"ORc": np.ascontiguousarray(np.concatenate([r0["o_OR"][:, sl], r1["o_OR"][:, sl]], axis=0)),
                     "OFc": np.ascontiguousarray(np.concatenate([r0["o_OF"][:, sl], r1["o_OF"][:, sl]], axis=0)),
                     "OMc": np.ascontiguousarray(np.concatenate([r0["o_OM"][:, sl], r1["o_OM"][:, sl]], axis=0))})
    return outs


def kernel(**inputs):
    inp = inputs
    x = np.asarray(inp["x"])
    positions = np.asarray(inp["positions"])
    ident = np.eye(128, dtype=np.float32)

    def pos_of(c):
        b, half = c // 2, c % 2
        return np.ascontiguousarray(positions[b, half * T:(half + 1) * T][None, :].astype(np.int32))

    hTs = None
    out = np.empty((4, S, D), np.float32)
    for l in range(2):
        shared = {"c_ident": ident}
        shared.update(_ffn_inputs(inp, l, "ffn1"))
        shared.update(_mixa_inputs(inp, l))
        maps = []
        for c in range(NCORES):
            b, half = c // 2, c % 2
            m = dict(shared)
            if l == 0:
                m["x"] = np.ascontiguousarray(x[b, half * T:(half + 1) * T])
            else:
                m["hT_in"] = hTs[c]
            m["pos"] = pos_of(c)
            maps.append(m)
        resA = _run("A0" if l == 0 else "A1", maps)
        hTs = [r["hT"] for r in resA]
        del maps, shared
        resB = _run("B", _regroup_b(resA, inp, l))
        oc = _regroup_c(resB)
        del resA, resB
        shared = {"c_ident": ident}
        shared.update(_mixc_inputs(inp, l))
        shared.update(_ffn_inputs(inp, l, "ffn2"))
        if l == 1:
            shared["final_g"] = pvec(inp["final_norm"])
        maps = []
        for c in range(NCORES):
            m = dict(shared)
            m["hT_in"] = hTs[c]
            m.update(oc[c])
            maps.append(m)
        resC = _run("CM" if l == 0 else "C1", maps)
        hTs = [r["hT"] for r in resC]
        del maps, shared
        if l == 1:
            for c in range(NCORES):
                b, half = c // 2, c % 2
                out[b, half * T:(half + 1) * T] = resC[c]["out"]
    return out
```

```python
from contextlib import ExitStack
import numpy as np
import ml_dtypes
import concourse.bass as bass
import concourse.mybir as mybir
from concourse.bass_utils import run_bass_kernel_spmd

F32 = mybir.dt.float32
BF16 = mybir.dt.bfloat16
I32 = mybir.dt.int32
AF = mybir.ActivationFunctionType
ALU = mybir.AluOpType

D = 4096
DFF = 8192
T = 2048
S = 4096
TT = 512
NT = T // TT
KC = D // 128
EPS = 1e-6
INW = 10824
C_RQ, C_RK, C_RV, C_RG = 0, 1024, 2048, 4096
C_FQ, C_FK, C_FV, C_FF = 6144, 7168, 8192, 9216
C_CQ, C_CKV, C_KR = 9224, 10248, 10760
SLAB = 8192
NSLAB = 4


class Tok:
    __slots__ = ("sem", "v")

    def __init__(self, sem, v):
        self.sem = sem
        self.v = v


def _flat(xs):
    for x in xs:
        if x is None:
            continue
        if isinstance(x, (list, tuple)):
            yield from _flat(x)
        else:
            yield x


class Eng:
    def __init__(self, k, name, eng):
        self.k = k
        self.name = name
        self.eng = eng
        self.sem = k.new_sem("e_" + name)
        self.cnt = 0
        self.seen = {}
        self.last = None
        self.last_tok = None

    def __call__(self, ins):
        self.last = ins
        self.last_tok = None
        return ins

    def m(self, ins):
        ins.then_inc(self.sem, 1)
        self.cnt += 1
        self.last = ins
        self.last_tok = Tok(self.sem, self.cnt)
        return self.last_tok

    def wait(self, *toks):
        for t in _flat(toks):
            if t.sem is self.sem:
                continue
            key = id(t.sem)
            if self.seen.get(key, -1) >= t.v:
                continue
            self.eng.wait_ge(t.sem, t.v)
            self.seen[key] = t.v

    def sw(self, tok):
        self.eng.wait_ge(tok.sem, tok.v)

    def tail(self):
        if self.last is None:
            return None
        if self.last_tok is None:
            self.m(self.last)
        return self.last_tok


class _DSem:
    def __init__(self, k, name):
        self.sem = k.new_sem("d_" + name)
        self.cnt = 0
        k.dsems.append(self)

    def dma(self, q, out, in_, **kw):
        ins = q.eng.dma_start(out=out, in_=in_, **kw)
        ins.then_inc(self.sem, 16)
        self.cnt += 16
        return Tok(self.sem, self.cnt)

    def tok(self):
        return Tok(self.sem, self.cnt) if self.cnt else None


def DSem(k, name):
    if k.ds_pool:
        d = k.ds_pool.pop()
    else:
        d = _DSem(k, name)
    if k.phase_es is not None:
        k.phase_ds.append(d)
    return d


class Ring:
    def __init__(self, k, name, n, shape, dtype, dma=True, psum=False):
        self.n = n
        self.i = 0
        self.bufs = []
        self.ds = []
        self.free = [None] * n
        for j in range(n):
            if psum:
                self.bufs.append(k.psum(f"{name}{j}", shape, dtype))
            else:
                self.bufs.append(k.sbuf(f"{name}{j}", shape, dtype))
            self.ds.append(DSem(k, f"{name}{j}") if dma else None)

    def next(self):
        j = self.i % self.n
        self.i += 1
        return j


class K:
    def __init__(self):
        self.nc = bass.Bass("TRN2", target_bir_lowering=False)
        self.es = ExitStack()
        self.phase_es = None
        self.dsems = []
        self.ds_pool = []
        self.phase_ds = []
        self.nsem = 0
        self.uid = 0
        nc = self.nc
        self.pe = Eng(self, "pe", nc.tensor)
        self.act = Eng(self, "act", nc.scalar)
        self.dve = Eng(self, "dve", nc.vector)
        self.pool = Eng(self, "pool", nc.gpsimd)
        self.sp = Eng(self, "sp", nc.sync)
        self.engs = [self.pe, self.act, self.dve, self.pool, self.sp]
        self.dram = {}
        self.in_names = []
        self.out_names = []

    def new_sem(self, name):
        self.nsem += 1
        return self.es.enter_context(self.nc.semaphore(name))

    def sbuf(self, name, shape, dtype):
        st = self.phase_es if self.phase_es is not None else self.es
        self.uid += 1
        return st.enter_context(self.nc.sbuf_tensor(f"{name}_{self.uid}", list(shape), dtype))

    def psum(self, name, shape, dtype):
        st = self.phase_es if self.phase_es is not None else self.es
        self.uid += 1
        return st.enter_context(self.nc.psum_tensor(f"{name}_{self.uid}", list(shape), dtype))

    def dt(self, name, shape, dtype, kind):
        t = self.nc.dram_tensor(name, list(shape), dtype, kind=kind).ap()
        self.dram[name] = t
        if kind == "ExternalInput":
            self.in_names.append(name)
        elif kind == "ExternalOutput":
            self.out_names.append(name)
        return t

    def barrier(self):
        toks = [e.tail() for e in self.engs]
        toks += [d.tok() for d in self.dsems]
        for e in self.engs:
            e.wait(toks)

    def begin_phase(self):
        self.phase_es = ExitStack()

    def end_phase(self):
        self.barrier()
        self.phase_es.close()
        self.phase_es = None
        self.ds_pool.extend(self.phase_ds)
        self.phase_ds = []


class Common:
    def __init__(self, k, cst):
        nc = k.nc
        self.k = k
        self.ident = k.sbuf("ident", [128, 128], F32)
        self.ones_f = k.sbuf("ones_f", [128, 128], F32)
        self.ones_b = k.sbuf("ones_b", [128, 128], BF16)
        self.ident_b = k.sbuf("ident_b", [128, 128], BF16)
        ds = DSem(k, "cst")
        t = ds.dma(k.sp, self.ident[:, :], cst["ident"][:, :])
        k.dve.wait(t)
        k.dve(nc.vector.memset(self.ones_f[:, :], 1.0))
        k.dve(nc.vector.memset(self.ones_b[:, :], 1.0))
        k.dve(nc.vector.tensor_copy(out=self.ident_b[:, :], in_=self.ident[:, :]))
        self.cst_tok = k.dve.tail()
        for e in (k.pe, k.act):
            e.wait(self.cst_tok)
        self.ps = [k.psum(f"ps{i}", [128, 512], F32) for i in range(8)]
        self.ps_free = [None] * 8
        self.wr = Ring(k, "wsl", NSLAB, [128, SLAB], BF16)

    def wload(self, pieces):
        k = self.k
        j = self.wr.next()
        tl = self.wr.bufs[j]
        k.pool.wait(self.wr.free[j])
        tok = None
        for dst_fn, src in pieces:
            tok = self.wr.ds[j].dma(k.pool, dst_fn(tl), src, max_dma_last_dim=8192)
        return j, tl, tok

    def wfree(self, j, tok):
        self.wr.free[j] = tok


def wview(w2d, kc0, nkc, c0, ncol):
    v = w2d.rearrange("(c p) n -> p c n", p=128)
    return v[:, kc0:kc0 + nkc, c0:c0 + ncol]


def slab_view(tl, nkc, ncol):
    return tl[:, 0:nkc * ncol].rearrange("p (c n) -> p c n", n=ncol)


def norm_u(k, cm, hT, t, gain_sb, uT, stage, out_f32=None):
    nc = k.nc
    pe, act, dve, sp = k.pe, k.act, k.dve, k.sp
    cols = slice(t * TT, (t + 1) * TT)
    PSS = 7
    pe.wait(cm.ps_free[PSS])
    sq = norm_u.sq
    last_mm = None
    for c in range(KC):
        j = stage.next()
        sp.wait(stage.free[j])
        tk = stage.ds[j].dma(sp, stage.bufs[j][:, :], hT[c * 128:(c + 1) * 128, cols])
        js = sq.next()
        act.wait(tk, sq.free[js])
        ta = act.m(nc.scalar.activation(out=sq.bufs[js][:, :], in_=stage.bufs[j][:, :], func=AF.Square))
        stage.free[j] = ta
        pe.wait(ta)
        last_mm = nc.tensor.matmul(cm.ps[PSS][:, :], lhsT=cm.ones_f[:, :], rhs=sq.bufs[js][:, :],
                                   start=(c == 0), stop=(c == KC - 1))
        sq.free[js] = pe.m(last_mm)
    tss = pe.last_tok
    rstd = norm_u.rstd
    act.wait(tss, norm_u.rstd_free)
    ta = act.m(nc.scalar.activation(out=rstd[:, :], in_=cm.ps[PSS][:, :], func=AF.Sqrt,
                                    scale=1.0 / D, bias=norm_u.eps_t[:, 0:1]))
    cm.ps_free[PSS] = ta
    dve.wait(ta)
    dve(nc.vector.reciprocal(out=rstd[:, :], in_=rstd[:, :]))
    last = None
    for c in range(KC):
        j = stage.next()
        sp.wait(stage.free[j])
        tk = stage.ds[j].dma(sp, stage.bufs[j][:, :], hT[c * 128:(c + 1) * 128, cols])
        dve.wait(tk)
        if out_f32 is None:
            if c == 0:
                dve.wait(norm_u.u_free)
            last = dve.m(nc.vector.scalar_tensor_tensor(
                out=uT[:, c, :], in0=stage.bufs[j][:, :], scalar=gain_sb[:, c:c + 1], in1=rstd[:, :],
                op0=ALU.mult, op1=ALU.mult))
            stage.free[j] = last
        else:
            last = out_f32(c, stage.bufs[j], rstd)
            stage.free[j] = last
    norm_u.rstd_free = last
    return last


def setup_norm(k):
    norm_u.sq = Ring(k, "nsq", 2, [128, TT], F32, dma=False)
    norm_u.rstd = k.sbuf("rstd", [128, TT], F32)
    norm_u.eps_t = k.sbuf("eps_t", [128, 1], F32)
    k.dve(k.nc.vector.memset(norm_u.eps_t[:, :], EPS))
    k.act.wait(k.dve.tail())
    norm_u.rstd_free = None
    norm_u.u_free = None


def phase_p0(k, cm, x, hT):
    nc = k.nc
    pe, act, dve, sp = k.pe, k.act, k.dve, k.sp
    k.begin_phase()
    xs = Ring(k, "p0x", 8, [128, D], F32)
    st = Ring(k, "p0s", 4, [128, TT], F32)
    pi = 0
    for t in range(NT):
        xt = []
        for b in range(4):
            j = xs.next()
            sp.wait(xs.free[j])
            r0 = t * TT + b * 128
            tk = xs.ds[j].dma(sp, xs.bufs[j][:, :], x[r0:r0 + 128, :])
            xt.append((j, tk))
        for c in range(KC):
            pb = pi % 4
            pi += 1
            pe.wait(cm.ps_free[pb])
            for b in range(4):
                j, tk = xt[b]
                pe.wait(tk)
                mm = nc.tensor.matmul(cm.ps[pb][:, b * 128:(b + 1) * 128],
                                      lhsT=xs.bufs[j][:, c * 128:(c + 1) * 128], rhs=cm.ident[:, :],
                                      start=True, stop=True)
            tp = pe.m(mm)
            if c == KC - 1:
                for b in range(4):
                    xs.free[xt[b][0]] = tp
            js = st.next()
            eng, e = (act, nc.scalar) if c % 2 == 0 else (dve, nc.vector)
            eng.wait(tp, st.ds[js].tok())
            if eng is act:
                te = act.m(nc.scalar.copy(out=st.bufs[js][:, :], in_=cm.ps[pb][:, :]))
            else:
                te = dve.m(nc.vector.tensor_copy(out=st.bufs[js][:, :], in_=cm.ps[pb][:, :]))
            cm.ps_free[pb] = te
            sp.wait(te)
            st.ds[js].dma(sp, hT[c * 128:(c + 1) * 128, t * TT:(t + 1) * TT], st.bufs[js][:, :])
    k.end_phase()


def phase_ffn(k, cm, hT, gain, w13, w2):
    nc = k.nc
    pe, act, dve, sp = k.pe, k.act, k.dve, k.sp
    k.begin_phase()
    setup_norm(k)
    TW = 2 * TT
    HQ = 4
    HC = DFF // 128 // HQ
    g_sb = k.sbuf("ffn_g", [128, KC], F32)
    gds = DSem(k, "ffn_g")
    dve.wait(gds.dma(sp, g_sb[:, :], gain[:, :]))
    uT = k.sbuf("ffn_u", [128, KC, TW], BF16)
    hid = k.sbuf("ffn_hid", [128, HC, TW], BF16)
    stage = Ring(k, "ffn_st", 4, [128, TT], F32)
    sa = Ring(k, "ffn_sa", 2, [128, TT], F32, dma=False)
    res = Ring(k, "ffn_res", 4, [128, TT], F32)
    hid_free = None
    pi = 0
    for t in range(T // TW):
        for sub in range(2):
            tu = norm_u(k, cm, hT, t * 2 + sub, g_sb, uT[:, :, sub * TT:(sub + 1) * TT], stage)
        pe.wait(tu)
        wtok = {}
        for q in range(HQ):
            for s in range(HC // 2):
                ca = q * HC * 128 + s * 256
                ja, ta_, tka = cm.wload([(lambda tl: slab_view(tl, KC, 256), wview(w13, 0, KC, ca, 256))])
                jb, tb_, tkb = cm.wload([(lambda tl: slab_view(tl, KC, 256), wview(w13, 0, KC, DFF + ca, 256))])
                va = slab_view(ta_, KC, 256)
                vb = slab_view(tb_, KC, 256)
                pe.wait(tka, tkb)
                for cc in range(2):
                    n = s * 2 + cc
                    for sub in range(2):
                        usl = slice(sub * TT, (sub + 1) * TT)
                        pa = (pi % 2) * 2
                        pb = pa + 1
                        pi += 1
                        pe.wait(cm.ps_free[pa], cm.ps_free[pb])
                        for kc in range(KC):
                            mm = nc.tensor.matmul(cm.ps[pa][:, :], lhsT=va[:, kc, cc * 128:(cc + 1) * 128], rhs=uT[:, kc, usl],
                                                  start=(kc == 0), stop=(kc == KC - 1))
                        tpa = pe.m(mm)
                        for kc in range(KC):
                            mm = nc.tensor.matmul(cm.ps[pb][:, :], lhsT=vb[:, kc, cc * 128:(cc + 1) * 128], rhs=uT[:, kc, usl],
                                                  start=(kc == 0), stop=(kc == KC - 1))
                        tpb = pe.m(mm)
                        js = sa.next()
                        act.wait(tpa, sa.free[js])
                        tsa = act.m(nc.scalar.activation(out=sa.bufs[js][:, :], in_=cm.ps[pa][:, :], func=AF.Silu))
                        cm.ps_free[pa] = tsa
                        dve.wait(tsa, tpb)
                        if n == 0 and sub == 0:
                            dve.wait(hid_free)
                        th = dve.m(nc.vector.tensor_tensor(out=hid[:, n, usl], in0=cm.ps[pb][:, :], in1=sa.bufs[js][:, :],
                                                           op=ALU.mult))
                        sa.free[js] = th
                        cm.ps_free[pb] = th
                cm.wfree(ja, tpb)
                cm.wfree(jb, tpb)
            if q == HQ - 1:
                norm_u.u_free = pe.last_tok
            pe.wait(k.dve.last_tok)
            for s in range(D // 512):
                j_, tl_, tk_ = cm.wload([(lambda tl: slab_view(tl, HC, 512), wview(w2, q * HC, HC, s * 512, 512))])
                v_ = slab_view(tl_, HC, 512)
                pe.wait(tk_)
                for cc in range(4):
                    n = s * 4 + cc
                    for sub in range(2):
                        usl = slice(sub * TT, (sub + 1) * TT)
                        csl = slice(t * TW + sub * TT, t * TW + (sub + 1) * TT)
                        pb_ = pi % 4
                        pi += 1
                        pe.wait(cm.ps_free[pb_])
                        for kc in range(HC):
                            mm = nc.tensor.matmul(cm.ps[pb_][:, :], lhsT=v_[:, kc, cc * 128:(cc + 1) * 128], rhs=hid[:, kc, usl],
                                                  start=(kc == 0), stop=(kc == HC - 1))
                        tcc = pe.m(mm)
                        j = stage.next()
                        sp.wait(stage.free[j], wtok.get((n, sub)))
                        tk = stage.ds[j].dma(sp, stage.bufs[j][:, :], hT[n * 128:(n + 1) * 128, csl])
                        jr = res.next()
                        dve.wait(tk, tcc, res.ds[jr].tok())
                        tr = dve.m(nc.vector.scalar_tensor_tensor(
                            out=res.bufs[jr][:, :], in0=cm.ps[pb_][:, :], scalar=0.5, in1=stage.bufs[j][:, :],
                            op0=ALU.mult, op1=ALU.add))
                        stage.free[j] = tr
                        cm.ps_free[pb_] = tr
                        sp.wait(tr)
                        wtok[(n, sub)] = res.ds[jr].dma(sp, hT[n * 128:(n + 1) * 128, csl], res.bufs[jr][:, :])
                cm.wfree(j_, tcc)
            hid_free = pe.last_tok
    k.end_phase()


_bank_rot = [0]


_BSETS = ((0, 1, 2, 3), (4, 5, 6, 7))


def linear_fm(k, cm, src, nkc, w2d, c0, ncols, epi, tw=TT, banks=None, sw=None, lhs_fn=None):
    nc = k.nc
    pe = k.pe
    w = 512 if ncols % 512 == 0 else (256 if ncols % 256 == 0 else 128)
    kg = min(nkc, SLAB // w)
    ng = (nkc + kg - 1) // kg
    assert nkc % kg == 0
    ncc = w // 128
    for g0 in range(0, ncols, w):
        bset = _BSETS[_bank_rot[0] % 2]
        _bank_rot[0] += 1
        for cc in range(ncc):
            pe.wait(cm.ps_free[bset[cc]])
        for g in range(ng):
            j, tl, tk = cm.wload([(lambda tl_: slab_view(tl_, kg, w), wview(w2d, g * kg, kg, c0 + g0, w))])
            v = slab_view(tl, kg, w)
            pe.wait(tk)
            tp = None
            for cc in range(ncc):
                b = bset[cc]
                for kc in range(kg):
                    mm = nc.tensor.matmul(cm.ps[b][:, 0:tw], lhsT=v[:, kc, cc * 128:(cc + 1) * 128], rhs=src(g * kg + kc),
                                          start=(g == 0 and kc == 0), stop=(g == ng - 1 and kc == kg - 1))
                tp = pe.m(mm)
                if g == ng - 1:
                    cm.ps_free[b] = epi(g0 // 128 + cc, cm.ps[b], tp)
            cm.wfree(j, tp)


def linear_tm(k, cm, uT, nkc, wsrc, ncols, epi, rhs_fn=None, slab_cols=None):
    nc = k.nc
    pe = k.pe
    sc = slab_cols or ncols
    kg = min(nkc, SLAB // sc)
    for tb in range(4):
        pe.wait(cm.ps_free[tb])
    tp = None
    ng = nkc // kg
    for g in range(ng):
        j, tl, tk = cm.wload([(lambda tl_: slab_view(tl_, kg, sc), wsrc(g * kg, kg))])
        v = slab_view(tl, kg, sc)
        pe.wait(tk)
        for tb in range(4):
            for kc in range(kg):
                rhs = v[:, kc, 0:ncols] if rhs_fn is None else rhs_fn(tl, kg, kc)
                mm = nc.tensor.matmul(cm.ps[tb][:, 0:ncols], lhsT=uT[:, g * kg + kc, tb * 128:(tb + 1) * 128], rhs=rhs,
                                      start=(g == 0 and kc == 0), stop=(g == ng - 1 and kc == kg - 1))
        tp = pe.m(mm)
        cm.wfree(j, tp)
    for tb in range(4):
        cm.ps_free[tb] = epi(tb, cm.ps[tb], tp)


class OutStage:
    def __init__(self, k, name, n, shape, dtype):
        self.k = k
        self.r = Ring(k, name, n, shape, dtype)
        self.flip = 0

    def slot(self, eng):
        j = self.r.next()
        eng.wait(self.r.ds[j].tok())
        return j, self.r.bufs[j]

    def store(self, j, tok, pieces):
        self.k.sp.wait(tok)
        for dst, src in pieces:
            self.r.ds[j].dma(self.k.sp, dst, src)

    def copy_epi(self, dst_fn, func=None, parts=128, tw=TT):
        k = self.k
        nc = k.nc

        def epi(n, ps, tok):
            use_act = (func is not None) or (self.flip % 2 == 0)
            self.flip += 1
            eng = k.act if use_act else k.dve
            j, buf = self.slot(eng)
            eng.wait(tok)
            if use_act and func is not None:
                te = k.act.m(nc.scalar.activation(out=buf[0:parts, 0:tw], in_=ps[0:parts, 0:tw], func=func))
            elif use_act:
                te = k.act.m(nc.scalar.copy(out=buf[0:parts, 0:tw], in_=ps[0:parts, 0:tw]))
            else:
                te = k.dve.m(nc.vector.tensor_copy(out=buf[0:parts, 0:tw], in_=ps[0:parts, 0:tw]))
            self.store(j, te, [(dst_fn(n), buf[0:parts, 0:tw])])
            return te
        return epi


PI = float(np.pi)


def phase_tables(k, cm, pos, inv, tabs):
    nc = k.nc
    act, dve, sp = k.act, k.dve, k.sp
    k.begin_phase()
    posi = k.sbuf("tb_posi", [128, T], I32)
    posf = k.sbuf("tb_posf", [128, T], F32)
    ang = k.sbuf("tb_ang", [128, T], F32)
    red = k.sbuf("tb_red", [128, T], F32)
    invs = k.sbuf("tb_inv", [128, 2], F32)
    npi = k.sbuf("tb_npi", [128, 1], F32)
    outb = Ring(k, "tb_o", 2, [128, T], F32)
    ds = DSem(k, "tb_in")
    ds.dma(sp, posi[:, :], pos.partition_broadcast(128))
    t0 = ds.dma(sp, invs[:, :], inv[:, :])
    dve.wait(t0)
    dve(nc.vector.memset(npi[:, :], -PI))
    dve(nc.vector.tensor_copy(out=posf[:, :], in_=posi[:, :]))
    ki = k.sbuf("tb_ki", [128, T], I32)
    kf = k.sbuf("tb_kf", [128, T], F32)
    C1 = 6.28125
    C2 = 2.0 * PI - 6.28125
    for ti, (col, nm_c, nm_s) in enumerate(((0, "cosR", "sinR"), (1, "cosM", "sinM"))):
        for nm, shift in ((nm_s, 0.0), (nm_c, 0.5 * PI)):
            dve(nc.vector.tensor_scalar(out=ang[:, :], in0=posf[:, :], scalar1=invs[:, col:col + 1], scalar2=shift,
                                        op0=ALU.mult, op1=ALU.add))
            dve(nc.vector.tensor_scalar(out=kf[:, :], in0=ang[:, :], scalar1=1.0 / (2.0 * PI), scalar2=None, op0=ALU.mult))
            dve(nc.vector.tensor_copy(out=ki[:, :], in_=kf[:, :]))
            dve(nc.vector.tensor_copy(out=kf[:, :], in_=ki[:, :]))
            dve(nc.vector.scalar_tensor_tensor(out=ang[:, :], in0=kf[:, :], scalar=-C1, in1=ang[:, :],
                                               op0=ALU.mult, op1=ALU.add))
            dve(nc.vector.scalar_tensor_tensor(out=ang[:, :], in0=kf[:, :], scalar=-C2, in1=ang[:, :],
                                               op0=ALU.mult, op1=ALU.add))
            dve(nc.vector.tensor_scalar(out=kf[:, :], in0=ang[:, :], scalar1=PI, scalar2=2.0 * PI,
                                        op0=ALU.is_gt, op1=ALU.mult))
            if ti or nm == nm_c:
                dve.wait(k.act.last_tok)
            td = dve.m(nc.vector.tensor_tensor(out=red[:, :], in0=ang[:, :], in1=kf[:, :], op=ALU.subtract))
            j = outb.next()
            act.wait(td, outb.ds[j].tok())
            ta = act.m(nc.scalar.activation(out=outb.bufs[j][:, :], in_=red[:, :], func=AF.Sin))
            sp.wait(ta)
            outb.ds[j].dma(sp, tabs[nm][:, :], outb.bufs[j][:, :])
    k.end_phase()


def phase_mixa(k, cm, hT, W, tabs, O):
    nc = k.nc
    pe, act, dve, sp = k.pe, k.act, k.dve, k.sp
    k.begin_phase()
    setup_norm(k)
    w_in = W["w_in"]
    g_sb = k.sbuf("ma_g", [128, KC], F32)
    bfg = k.sbuf("ma_bfg", [128, 8], F32)
    qng = k.sbuf("ma_qng", [128, 8], F32)
    kvg = k.sbuf("ma_kvg", [128, 4], F32)
    one_t = k.sbuf("ma_one", [128, 1], F32)
    pds = DSem(k, "ma_p")
    pds.dma(sp, g_sb[:, :], W["mix_g"][:, :])
    pds.dma(sp, bfg[:, :], W["bfg"][:, :])
    pds.dma(sp, qng[:, :], W["qn_g"][:, :])
    tparam = pds.dma(sp, kvg[:, :], W["kvn_g"][:, :])
    dve.wait(tparam)
    act.wait(tparam)
    dve(nc.vector.memset(one_t[:, :], 1.0))
    act.wait(dve.tail())
    uT = k.sbuf("ma_u", [128, KC, TT], BF16)
    stage = Ring(k, "ma_st", 3, [128, TT], F32)
    tabr = [k.sbuf(f"ma_tab{i}", [128, TT], F32) for i in range(4)]
    tab_ds = DSem(k, "ma_tab")
    tab_free = None
    tmp1 = k.sbuf("ma_tmp1", [128, TT], F32)
    tmp2 = k.sbuf("ma_tmp2", [128, TT], F32)
    ob = OutStage(k, "ma_ob", 6, [128, TT], BF16)
    cT = k.sbuf("ma_cT", [128, 8, TT], F32)
    cn = k.sbuf("ma_cn", [128, 8, TT], BF16)
    fls = Ring(k, "ma_fl", 2, [128, 8], F32)
    fx = k.sbuf("ma_fx", [128, 8], F32)
    fl2 = k.sbuf("ma_fl2", [128, 8], F32)
    cT_free = None
    cn_free = None

    def rope_epi(psA, psB, tA, tB, cosT, sinT, parts, stores):
        P = slice(0, parts)
        j1, b1 = ob.slot(dve)
        j2, b2 = ob.slot(dve)
        dve.wait(tA, tB)
        dve(nc.vector.tensor_tensor(out=tmp1[P, :], in0=psA[P, :], in1=cosT[P, :], op=ALU.mult))
        dve(nc.vector.tensor_tensor(out=tmp2[P, :], in0=psB[P, :], in1=sinT[P, :], op=ALU.mult))
        t1 = dve.m(nc.vector.tensor_tensor(out=b1[P, :], in0=tmp1[P, :], in1=tmp2[P, :], op=ALU.subtract))
        dve(nc.vector.tensor_tensor(out=tmp1[P, :], in0=psB[P, :], in1=cosT[P, :], op=ALU.mult))
        dve(nc.vector.tensor_tensor(out=tmp2[P, :], in0=psA[P, :], in1=sinT[P, :], op=ALU.mult))
        t2 = dve.m(nc.vector.tensor_tensor(out=b2[P, :], in0=tmp1[P, :], in1=tmp2[P, :], op=ALU.add))
        p1, p2 = stores(b1, b2)
        ob.store(j1, t1, p1)
        ob.store(j2, t2, p2)
        return t2

    rope_banks = [(4, 5), (2, 3)]
    rb = 0
    for t in range(NT):
        cols = slice(t * TT, (t + 1) * TT)
        tu = norm_u(k, cm, hT, t, g_sb, uT, stage)
        pe.wait(tu)
        sp.wait(tab_free)
        for i, nm in enumerate(("cosR", "sinR", "cosM", "sinM")):
            ttab = tab_ds.dma(sp, tabr[i][:, :], tabs[nm][:, cols])
        dve.wait(ttab)
        cosR, sinR, cosM, sinM = tabr

        def src_u(kc):
            return uT[:, kc, :]

        for dst, c_off in ((O["RQ"], C_RQ), (O["RK"], C_RK)):
            for p in range(4):
                pieces = []
                for two in range(2):
                    for hh in range(2):
                        pieces.append((lambda tl_, two=two, hh=hh: slab_view(tl_, KC, 256)[:, :, two * 128 + hh * 64:two * 128 + hh * 64 + 64],
                                       wview(w_in, 0, KC, c_off + p * 256 + hh * 128 + two * 64, 64)))
                j, tl, tk = cm.wload(pieces)
                v5 = tl[:, 0:KC * 256].rearrange("p (c two m) -> p c two m", two=2, m=128)
                bA, bB = rope_banks[rb % 2]
                rb += 1
                pe.wait(tk, cm.ps_free[bA], cm.ps_free[bB])
                for kc in range(KC):
                    mm = nc.tensor.matmul(cm.ps[bA][:, :], lhsT=v5[:, kc, 0, :], rhs=uT[:, kc, :],
                                          start=(kc == 0), stop=(kc == KC - 1))
                tA = pe.m(mm)
                for kc in range(KC):
                    mm = nc.tensor.matmul(cm.ps[bB][:, :], lhsT=v5[:, kc, 1, :], rhs=uT[:, kc, :],
                                          start=(kc == 0), stop=(kc == KC - 1))
                tB = pe.m(mm)
                cm.wfree(j, tB)

                def stores(b1, b2, p=p, dst=dst):
                    p1, p2 = [], []
                    for hh in range(2):
                        h = 2 * p + hh
                        p1.append((dst[h // 4, h % 4, 0:64, cols], b1[hh * 64:(hh + 1) * 64, :]))
                        p2.append((dst[h // 4, h % 4, 64:128, cols], b2[hh * 64:(hh + 1) * 64, :]))
                    return p1, p2
                te = rope_epi(cm.ps[bA], cm.ps[bB], tA, tB, cosR, sinR, 128, stores)
                cm.ps_free[bA] = te
                cm.ps_free[bB] = te
        for g in range(4):
            def epi_rv(tb, ps, tok, g=g):
                return ob.copy_epi(lambda n: O["RV"][g // 2, t * TT + tb * 128:t * TT + (tb + 1) * 128,
                                                     (g % 2) * 512:(g % 2) * 512 + 512])(tb, ps, tok)
            linear_tm(k, cm, uT, KC, lambda kc0, n, g=g: wview(w_in, kc0, n, C_RV + g * 512, 512), 512, epi_rv)
        linear_fm(k, cm, src_u, KC, w_in, C_RG, 2048,
                  ob.copy_epi(lambda n: O["RG"][n // 8, (n % 8) * 128:(n % 8) * 128 + 128, cols], func=AF.Silu))
        linear_fm(k, cm, src_u, KC, w_in, C_FQ, 1024, ob.copy_epi(lambda n: O["FQ"][n // 4, n % 4, :, cols]))
        linear_fm(k, cm, src_u, KC, w_in, C_FK, 1024, ob.copy_epi(lambda n: O["FK"][n // 4, n % 4, :, cols]))
        for g in range(2):
            def epi_fv(tb, ps, tok, g=g):
                return ob.copy_epi(lambda n: O["FV"][g, t * TT + tb * 128:t * TT + (tb + 1) * 128, :])(tb, ps, tok)
            linear_tm(k, cm, uT, KC, lambda kc0, n, g=g: wview(w_in, kc0, n, C_FV + g * 512, 512), 512, epi_fv)
        def epi_ff(tb, ps, tok):
            j = fls.next()
            dve.wait(tok, fls.ds[j].tok())
            td = dve.m(nc.vector.tensor_tensor(out=fx[:, :], in0=ps[:, 0:8], in1=bfg[:, :], op=ALU.add))
            act.wait(td)
            t1 = act.m(nc.scalar.activation(out=fls.bufs[j][:, :], in_=fx[:, :], func=AF.Exp, scale=-1.0))
            act.sw(t1)
            t2 = act.m(nc.scalar.activation(out=fl2[:, :], in_=fls.bufs[j][:, :], func=AF.Ln, bias=one_t[:, 0:1]))
            act.sw(t2)
            ta = act.m(nc.scalar.mul(out=fls.bufs[j][:, :], in_=fl2[:, :], mul=-1.0))
            dve.wait(ta)
            sp.wait(ta)
            fls.ds[j].dma(sp, O["FL"][t * TT + tb * 128:t * TT + (tb + 1) * 128, :], fls.bufs[j][:, :])
            return td
        linear_tm(k, cm, uT, KC, lambda kc0, n: wview(w_in, kc0, n, C_FF, 8), 8, epi_ff)

        def latent_norm(c_off, nch, gains):
            nonlocal cT_free, cn_free
            def epi_c(n, ps, tok):
                eng = act if n % 2 == 0 else dve
                eng.wait(tok)
                if n < 2:
                    eng.wait(cT_free)
                if eng is act:
                    return act.m(nc.scalar.copy(out=cT[:, n, :], in_=ps[:, :]))
                return dve.m(nc.vector.tensor_copy(out=cT[:, n, :], in_=ps[:, :]))
            linear_fm(k, cm, src_u, KC, w_in, c_off, nch * 128, epi_c)
            tc_a, tc_d = act.last_tok, dve.last_tok
            sq = norm_u.sq
            PSS = 7
            pe.wait(cm.ps_free[PSS])
            for n in range(nch):
                js = sq.next()
                act.wait(tc_a, tc_d, sq.free[js])
                ta = act.m(nc.scalar.activation(out=sq.bufs[js][:, :], in_=cT[:, n, :], func=AF.Square))
                pe.wait(ta)
                mm = nc.tensor.matmul(cm.ps[PSS][:, :], lhsT=cm.ones_f[:, :], rhs=sq.bufs[js][:, :],
                                      start=(n == 0), stop=(n == nch - 1))
                sq.free[js] = pe.m(mm)
            rstd = norm_u.rstd
            act.wait(pe.last_tok, norm_u.rstd_free)
            ta = act.m(nc.scalar.activation(out=rstd[:, :], in_=cm.ps[PSS][:, :], func=AF.Sqrt,
                                            scale=1.0 / (nch * 128), bias=norm_u.eps_t[:, 0:1]))
            cm.ps_free[PSS] = ta
            dve.wait(ta, tc_a, cn_free)
            dve(nc.vector.reciprocal(out=rstd[:, :], in_=rstd[:, :]))
            for n in range(nch):
                last = dve.m(nc.vector.scalar_tensor_tensor(out=cn[:, n, :], in0=cT[:, n, :], scalar=gains[:, n:n + 1],
                                                            in1=rstd[:, :], op0=ALU.mult, op1=ALU.mult))
            norm_u.rstd_free = last
            cT_free = last
            return last

        tcn = latent_norm(C_CQ, 8, qng)
        pe.wait(tcn)
        w_uq = W["w_uq"]
        for hg in range(2):
            pieces = [(lambda tl_: slab_view(tl_, 8, 768), wview(w_uq, 0, 8, hg * 768, 768))]
            for two in range(2):
                for hl in range(4):
                    pieces.append((lambda tl_, two=two, hl=hl: tl_[:, 6144 + two * 1024:6144 + (two + 1) * 1024].rearrange(
                        "p (c m) -> p c m", m=128)[:, :, hl * 32:(hl + 1) * 32],
                        wview(w_uq, 0, 8, hg * 768 + hl * 192 + 128 + two * 32, 32)))
            j, tl, tk = cm.wload(pieces)
            v4 = tl[:, 0:8 * 768].rearrange("p (c h f) -> p c h f", h=4, f=192)
            vpe = [tl[:, 6144 + two * 1024:6144 + (two + 1) * 1024].rearrange("p (c m) -> p c m", m=128) for two in range(2)]
            pe.wait(tk)
            for hl in range(4):
                b = (0, 1, 2, 3)[_bank_rot[0] % 4]
                _bank_rot[0] += 1
                pe.wait(cm.ps_free[b])
                for kc in range(8):
                    mm = nc.tensor.matmul(cm.ps[b][:, :], lhsT=v4[:, kc, hl, 0:128], rhs=cn[:, kc, :],
                                          start=(kc == 0), stop=(kc == 7))
                tp = pe.m(mm)
                cm.ps_free[b] = ob.copy_epi(lambda n, hg=hg, hl=hl: O["MQN"][hg, hl, :, cols])(0, cm.ps[b], tp)
            bA, bB = rope_banks[rb % 2]
            rb += 1
            pe.wait(cm.ps_free[bA], cm.ps_free[bB])
            for kc in range(8):
                mm = nc.tensor.matmul(cm.ps[bA][:, :], lhsT=vpe[0][:, kc, :], rhs=cn[:, kc, :],
                                      start=(kc == 0), stop=(kc == 7))
            tA = pe.m(mm)
            for kc in range(8):
                mm = nc.tensor.matmul(cm.ps[bB][:, :], lhsT=vpe[1][:, kc, :], rhs=cn[:, kc, :],
                                      start=(kc == 0), stop=(kc == 7))
            tB = pe.m(mm)
            cm.wfree(j, tB)

            def stores_q(b1, b2, hg=hg):
                p1 = [(O["MQP"][hg, hl, 0:32, cols], b1[hl * 32:(hl + 1) * 32, :]) for hl in range(4)]
                p2 = [(O["MQP"][hg, hl, 32:64, cols], b2[hl * 32:(hl + 1) * 32, :]) for hl in range(4)]
                return p1, p2
            te = rope_epi(cm.ps[bA], cm.ps[bB], tA, tB, cosM, sinM, 128, stores_q)
            cm.ps_free[bA] = te
            cm.ps_free[bB] = te
        cn_free = pe.last_tok
        tcn = latent_norm(C_CKV, 4, kvg)
        pe.wait(tcn)
        w_ukv = W["w_ukv"]
        j, tl, tk = cm.wload([(lambda tl_: slab_view(tl_, 4, 2048), wview(w_ukv, 0, 4, 0, 2048))])
        v4 = tl[:, 0:8192].rearrange("p (c h f) -> p c h f", h=8, f=256)
        pe.wait(tk)
        for h in range(8):
            b = (0, 1, 2, 3)[_bank_rot[0] % 4]
            _bank_rot[0] += 1
            pe.wait(cm.ps_free[b])
            for kc in range(4):
                mm = nc.tensor.matmul(cm.ps[b][:, :], lhsT=v4[:, kc, h, 0:128], rhs=cn[:, kc, :],
                                      start=(kc == 0), stop=(kc == 3))
            tp = pe.m(mm)
            cm.ps_free[b] = ob.copy_epi(lambda n, h=h: O["MKN"][h // 4, h % 4, :, cols])(0, cm.ps[b], tp)
        for hg in range(2):
            for tb in range(4):
                pe.wait(cm.ps_free[tb])
                for hl in range(4):
                    for kc in range(4):
                        mm = nc.tensor.matmul(cm.ps[tb][:, hl * 128:(hl + 1) * 128], lhsT=cn[:, kc, tb * 128:(tb + 1) * 128],
                                              rhs=v4[:, kc, hg * 4 + hl, 128:256], start=(kc == 0), stop=(kc == 3))
                tp = pe.m(mm)
                cm.ps_free[tb] = ob.copy_epi(
                    lambda n, hg=hg, tb=tb: O["MV"][hg, t * TT + tb * 128:t * TT + (tb + 1) * 128, :])(0, cm.ps[tb], tp)
        cm.wfree(j, pe.last_tok)
        cn_free = pe.last_tok
        j, tl, tk = cm.wload([(lambda tl_: slab_view(tl_, KC, 64), wview(w_in, 0, KC, C_KR, 64))])
        v = slab_view(tl, KC, 64)
        bA, bB = rope_banks[rb % 2]
        rb += 1
        pe.wait(tk, cm.ps_free[bA], cm.ps_free[bB])
        for kc in range(KC):
            mm = nc.tensor.matmul(cm.ps[bA][0:32, :], lhsT=v[:, kc, 0:32], rhs=uT[:, kc, :],
                                  start=(kc == 0), stop=(kc == KC - 1))
        tA = pe.m(mm)
        for kc in range(KC):
            mm = nc.tensor.matmul(cm.ps[bB][0:32, :], lhsT=v[:, kc, 32:64], rhs=uT[:, kc, :],
                                  start=(kc == 0), stop=(kc == KC - 1))
        tB = pe.m(mm)
        cm.wfree(j, tB)
        te = rope_epi(cm.ps[bA], cm.ps[bB], tA, tB, cosM, sinM, 32,
                      lambda b1, b2: ([(O["MKP"][0:32, cols], b1[0:32, :])], [(O["MKP"][32:64, cols], b2[0:32, :])]))
        cm.ps_free[bA] = te
        cm.ps_free[bB] = te
        tab_free = te
        norm_u.u_free = pe.last_tok
    k.end_phase()


NB = S // 128
NQT = S // TT


def attn_core(k, cm, name, Qd, Kd, Vd, Od, scale, mask_sb, bias_tab, pe_parts):
    nc = k.nc
    pe, act, dve, sp = k.pe, k.act, k.dve, k.sp
    LAG = 3
    SB = (0, 1, 2, 3)
    OB = ((4, 5), (6, 7))
    qs = [Ring(k, f"{name}q{i}", 2, [p_, S], BF16) for i, (_, p_) in enumerate(Qd)]
    ks = [Ring(k, f"{name}k{i}", 2, [p_, S], BF16) for i, (_, p_) in enumerate(Kd)]
    vs = Ring(k, name + "v", 2, [128, NB, 128], BF16)
    pr = Ring(k, name + "p", 6, [128, TT], BF16, dma=False)
    rec = k.sbuf(name + "rec", [128, TT], F32)
    ob = OutStage(k, name + "ob", 2, [128, TT], BF16)
    head_done = [None, None]
    si = [0]
    for hl in range(4):
        jb = hl % 2
        sp.wait(head_done[jb])
        for i, (qd, p_) in enumerate(Qd):
            qs[i].ds[jb].dma(sp, qs[i].bufs[jb][:, :], qd[hl, :, :])
        tq = [Tok(qs[i].ds[jb].sem, qs[i].ds[jb].cnt) for i in range(len(Qd))]
        tk_ = []
        for i, (kd, p_) in enumerate(Kd):
            src = kd[hl, :, :] if len(kd.shape) == 3 else kd[:, :]
            tk_.append(ks[i].ds[jb].dma(sp, ks[i].bufs[jb][:, :], src))
        tv = vs.ds[jb].dma(sp, vs.bufs[jb][:, :, :],
                           Vd[:, hl * 128:(hl + 1) * 128].rearrange("(j p) d -> p j d", p=128))
        pe.wait(tq, tk_, tv)
        items = [(t, j) for t in range(NQT) for j in range(4 * t + 4)]

        def stage_a(t, j):
            bs = SB[si[0] % len(SB)]
            si[0] += 1
            pe.wait(cm.ps_free[bs])
            for i in range(len(Qd)):
                mm = nc.tensor.matmul(cm.ps[bs][:, :], lhsT=ks[i].bufs[jb][:, j * 128:(j + 1) * 128],
                                      rhs=qs[i].bufs[jb][:, t * TT:(t + 1) * TT],
                                      start=(i == 0), stop=(i == len(Qd) - 1))
            ts = pe.m(mm)
            jj = j - 4 * t
            c0 = max(0, jj) * 128
            jp = pr.next()
            pb = pr.bufs[jp]
            act.wait(ts, pr.free[jp])
            if bias_tab is None:
                ta = act.m(nc.scalar.activation(out=pb[:, c0:TT], in_=cm.ps[bs][:, c0:TT], func=AF.Exp, scale=scale))
            else:
                for qb in range(max(0, jj), 4):
                    i_ = 4 * t + qb
                    ta = act.m(nc.scalar.activation(out=pb[:, qb * 128:(qb + 1) * 128],
                                                    in_=cm.ps[bs][:, qb * 128:(qb + 1) * 128], func=AF.Exp,
                                                    scale=scale, bias=bias_tab[:, i_, j, hl:hl + 1]))
            cm.ps_free[bs] = ta
            tp_ = ta
            if jj >= 0:
                dve.wait(ta)
                tp_ = dve.m(nc.vector.tensor_tensor(out=pb[:, c0:c0 + 128], in0=pb[:, c0:c0 + 128],
                                                    in1=mask_sb[:, :], op=ALU.mult))
            return jp, tp_, c0

        def stage_b(t, j, jp, tp_, c0):
            bo, bd = OB[t % 2]
            nj = 4 * t + 4
            pb = pr.bufs[jp]
            if j == 0:
                pe.wait(cm.ps_free[bo], cm.ps_free[bd])
            pe.wait(tp_)
            nc.tensor.matmul(cm.ps[bo][:, c0:TT], lhsT=vs.bufs[jb][:, j, :], rhs=pb[:, c0:TT],
                             start=(j == 0), stop=(j == nj - 1))
            mm = nc.tensor.matmul(cm.ps[bd][:, c0:TT], lhsT=cm.ones_b[:, :], rhs=pb[:, c0:TT],
                                  start=(j == 0), stop=(j == nj - 1))
            pr.free[jp] = pe.m(mm)
            if j == nj - 1:
                tacc = pe.last_tok
                dve.wait(tacc)
                dve(nc.vector.reciprocal(out=rec[:, :], in_=cm.ps[bd][:, :]))
                jo, obuf = ob.slot(dve)
                to = dve.m(nc.vector.tensor_tensor(out=obuf[:, :], in0=cm.ps[bo][:, :], in1=rec[:, :], op=ALU.mult))
                cm.ps_free[bo] = to
                cm.ps_free[bd] = to
                ob.store(jo, to, [(Od[hl * 128:(hl + 1) * 128, t * TT:(t + 1) * TT], obuf[:, :])])

        pend = []
        for idx in range(len(items) + LAG):
            if idx < len(items):
                t, j = items[idx]
                pend.append((t, j) + stage_a(t, j))
            if idx >= LAG:
                stage_b(*pend.pop(0))
        head_done[jb] = pe.last_tok


def phase_fox(k, cm, I, cst, Od):
    nc = k.nc
    pe, act, dve, sp = k.pe, k.act, k.dve, k.sp
    k.begin_phase()
    tri = k.sbuf("fx_tri", [128, 128], BF16)
    U = k.sbuf("fx_U", [128, 128], F32)
    M = k.sbuf("fx_M", [128, 128], F32)
    lf = k.sbuf("fx_lf", [128, NB, 4], F32)
    totT = k.sbuf("fx_totT", [128, 128], F32)
    F_sb = k.sbuf("fx_F", [128, NB, 4], F32)
    fc = k.sbuf("fx_fc", [128, NB, 4], F32)
    Ball = k.sbuf("fx_B", [128, NB, NB, 4], F32)
    ds = DSem(k, "fx_c")
    ds.dma(sp, tri[:, :], cst["tri"][:, :])
    ds.dma(sp, U[:, :], cst["U"][:, :])
    ds.dma(sp, M[:, :], cst["M"][:, :])
    with nc.allow_non_contiguous_dma(reason="tiny forget-gate table"):
        t0 = ds.dma(sp, lf[:, :, :], I["FL"].rearrange("(j p) h -> p j h", p=128))
    lf2 = lf[:, :, :].rearrange("p j h -> p (j h)")
    pe.wait(t0, cm.ps_free[0], cm.ps_free[1], cm.ps_free[2])
    t1 = pe.m(nc.tensor.matmul(cm.ps[0][:, 0:128], lhsT=lf2, rhs=cm.ones_f[:, :], start=True, stop=True))
    dve.wait(t1)
    t2 = dve.m(nc.vector.tensor_copy(out=totT[:, :], in_=cm.ps[0][:, 0:128]))
    pe.wait(t2)
    t3 = pe.m(nc.tensor.matmul(cm.ps[1][:, 0:128], lhsT=totT[:, :], rhs=M[:, :], start=True, stop=True))
    nc.tensor.matmul(cm.ps[2][:, 0:128], lhsT=U[:, :], rhs=lf2, start=True, stop=False)
    t4 = pe.m(nc.tensor.matmul(cm.ps[2][:, 0:128], lhsT=totT[:, :], rhs=M[:, :], start=False, stop=True))
    dve.wait(t3, t4)
    dve(nc.vector.tensor_copy(out=F_sb[:, :, :].rearrange("p j h -> p (j h)"), in_=cm.ps[1][:, 0:128]))
    t5 = dve.m(nc.vector.tensor_copy(out=fc[:, :, :].rearrange("p j h -> p (j h)"), in_=cm.ps[2][:, 0:128]))
    for b in range(3):
        cm.ps_free[b] = t5
    dve.sw(t5)
    last = None
    for i in range(NB):
        last = dve.m(nc.vector.tensor_tensor(out=Ball[:, i, 0:i + 1, :],
                                             in0=F_sb[:, i:i + 1, :].to_broadcast([128, i + 1, 4]),
                                             in1=fc[:, 0:i + 1, :], op=ALU.subtract))
    act.wait(last)
    attn_core(k, cm, "fx", [(I["FQ"], 128)], [(I["FK"], 128)], I["FV"], Od, 128.0 ** -0.5, tri, Ball, None)
    k.end_phase()


def phase_mla(k, cm, I, cst, Od):
    nc = k.nc
    k.begin_phase()
    cmask = k.sbuf("ml_cm", [128, 128], BF16)
    ds = DSem(k, "ml_c")
    t0 = ds.dma(k.sp, cmask[:, :], cst["cmask"][:, :])
    k.dve.wait(t0)
    attn_core(k, cm, "ml", [(I["MQN"], 128), (I["MQP"], 64)], [(I["MKN"], 128), (I["MKP"], 64)], I["MV"], Od,
              192.0 ** -0.5, cmask, None, None)
    k.end_phase()


def phase_ret(k, cm, I, cst, Oraw, Od):
    nc = k.nc
    pe, act, dve, sp = k.pe, k.act, k.dve, k.sp
    k.begin_phase()
    NCH = S // 64
    DmT = k.sbuf("rt_Dm", [128, 4, 64], F32)
    xi = k.sbuf("rt_xi", [128, 4, 64], F32)
    zeta = k.sbuf("rt_zeta", [128, 4], F32)
    gC = k.sbuf("rt_gC", [128, 4], F32)
    ds = DSem(k, "rt_c")
    ds.dma(sp, DmT[:, :, :], cst["DmT"][:, :, :])
    ds.dma(sp, xi[:, :, :], cst["xi"][:, :, :])
    ds.dma(sp, zeta[:, :], cst["zeta"][:, :])
    t0 = ds.dma(sp, gC[:, :], cst["gC"][:, :])
    dve.wait(t0)
    q_sb = [k.sbuf(f"rt_q{i}", [128, S], BF16) for i in range(2)]
    qx_sb = [k.sbuf(f"rt_qx{i}", [128, S], BF16) for i in range(2)]
    k_sb = [k.sbuf(f"rt_k{i}", [128, S], BF16) for i in range(2)]
    v_sb = [k.sbuf(f"rt_v{i}", [128, NB, 256], BF16) for i in range(2)]
    kz_sb = [k.sbuf(f"rt_kz{i}", [128, NB, 128], BF16) for i in range(2)]
    st_f = [k.sbuf(f"rt_sf{i}", [128, 256], F32) for i in range(2)]
    st_b = [Ring(k, f"rt_sb{i}", 2, [128, 256], BF16, dma=False) for i in range(2)]
    at_r = Ring(k, "rt_at", 4, [128, 64], BF16, dma=False)
    lds = [DSem(k, f"rt_l{i}") for i in range(2)]
    oraw = OutStage(k, "rt_or", 4, [128, TT], F32)
    grp_done = None
    for grp in range(2):
        heads = [grp * 2, grp * 2 + 1]
        sp.wait(grp_done)
        tl = []
        for i, hl in enumerate(heads):
            lds[i].dma(sp, q_sb[i][:, :], I["RQ"][hl, :, :])
            lds[i].dma(sp, k_sb[i][:, :], I["RK"][hl, :, :])
            tl.append(lds[i].dma(sp, v_sb[i][:, :, :],
                                 I["RV"][:, hl * 256:(hl + 1) * 256].rearrange("(j p) d -> p j d", p=128)))
        for i, hl in enumerate(heads):
            dve.wait(tl[i])
            for t in range(NQT):
                dve(nc.vector.tensor_tensor(
                    out=qx_sb[i][:, t * TT:(t + 1) * TT].rearrange("p (c f) -> p c f", f=64),
                    in0=q_sb[i][:, t * TT:(t + 1) * TT].rearrange("p (c f) -> p c f", f=64),
                    in1=xi[:, hl:hl + 1, :].to_broadcast([128, 8, 64]), op=ALU.mult))
            dve(nc.vector.memset(st_f[i][:, :], 0.0))
            dve(nc.vector.memset(st_b[i].bufs[0][:, :], 0.0))
            st_b[i].i = 0
            pe.wait(tl[i])
            for j in range(NB):
                b = j % 2
                pe.wait(cm.ps_free[b])
                tp = pe.m(nc.tensor.matmul(cm.ps[b][:, 0:128], lhsT=k_sb[i][:, j * 128:(j + 1) * 128],
                                           rhs=cm.ident_b[:, :], start=True, stop=True))
                dve.wait(tp)
                cm.ps_free[b] = dve.m(nc.vector.tensor_scalar(out=kz_sb[i][:, j, :], in0=cm.ps[b][:, 0:128],
                                                             scalar1=zeta[:, hl:hl + 1], scalar2=None, op0=ALU.mult))
        tprep = dve.tail()
        pe.wait(tprep)
        act.wait(tprep)
        st_tok = [tprep, tprep]
        stf_tok = [None, None]
        for n in range(NCH):
            par = n % 2
            pS, pO, pD = (0, 1, 2) if par == 0 else (3, 4, 5)
            j = n // 2
            hp = (n % 2) * 64
            P = slice(hp, hp + 64)
            cs = slice(n * 64, (n + 1) * 64)
            pe.wait(cm.ps_free[pS], cm.ps_free[pO], cm.ps_free[pD])
            tS = []
            for i in range(2):
                tS.append(pe.m(nc.tensor.matmul(cm.ps[pS][P, i * 64:(i + 1) * 64], lhsT=k_sb[i][:, cs], rhs=q_sb[i][:, cs],
                                                start=True, stop=True)))
            ats = []
            for i, hl in enumerate(heads):
                ja = at_r.next()
                dve.wait(tS[i], at_r.free[ja])
                ta = dve.m(nc.vector.tensor_tensor(out=at_r.bufs[ja][P, :], in0=cm.ps[pS][P, i * 64:(i + 1) * 64],
                                                   in1=DmT[P, hl, :], op=ALU.mult))
                ats.append((ja, ta))
            cm.ps_free[pS] = ats[-1][1]
            tO = []
            for i in range(2):
                ja, ta = ats[i]
                cur = st_b[i].bufs[st_b[i].i % 2]
                pe.wait(ta, st_tok[i])
                for c in range(2):
                    oc = (i * 2 + c) * 64
                    nc.tensor.matmul(cm.ps[pO][:, oc:oc + 64], lhsT=v_sb[i][P, j, c * 128:(c + 1) * 128],
                                     rhs=at_r.bufs[ja][P, :], start=True, stop=False)
                    mm = nc.tensor.matmul(cm.ps[pO][:, oc:oc + 64], lhsT=cur[:, c * 128:(c + 1) * 128],
                                          rhs=qx_sb[i][:, cs], start=False, stop=True)
                tO.append(pe.m(mm))
                at_r.free[ja] = tO[-1]
            tD = []
            for i in range(2):
                tD.append(pe.m(nc.tensor.matmul(cm.ps[pD][:, i * 256:(i + 1) * 256], lhsT=kz_sb[i][P, j, :],
                                                rhs=v_sb[i][P, j, :], start=True, stop=True)))
            for i, hl in enumerate(heads):
                dve.wait(tD[i], stf_tok[i])
                tf = dve.m(nc.vector.scalar_tensor_tensor(out=st_f[i][:, :], in0=st_f[i][:, :], scalar=gC[:, hl:hl + 1],
                                                          in1=cm.ps[pD][:, i * 256:(i + 1) * 256],
                                                          op0=ALU.mult, op1=ALU.add))
                st_b[i].i += 1
                nxt = st_b[i].bufs[st_b[i].i % 2]
                act.wait(tf, tO[i])
                ta = act.m(nc.scalar.copy(out=nxt[:, :], in_=st_f[i][:, :]))
                st_tok[i] = ta
                stf_tok[i] = ta
            cm.ps_free[pD] = dve.last_tok
            if n % 8 == 0:
                stg = [oraw.slot(act) for _ in range(4)]
            act.wait(tO[1])
            for i in range(2):
                for c in range(2):
                    oc = (i * 2 + c) * 64
                    ta = act.m(nc.scalar.copy(out=stg[i * 2 + c][1][:, (n % 8) * 64:(n % 8 + 1) * 64],
                                              in_=cm.ps[pO][:, oc:oc + 64]))
            cm.ps_free[pO] = ta
            if n % 8 == 7:
                tt_ = n // 8
                for i, hl in enumerate(heads):
                    for c in range(2):
                        jj_, buf = stg[i * 2 + c]
                        oraw.store(jj_, ta, [(Oraw[hl * 256 + c * 128:hl * 256 + (c + 1) * 128, tt_ * TT:(tt_ + 1) * TT], buf[:, :])])
        grp_done = pe.last_tok
    k.end_phase()
    k.begin_phase()
    setup_norm(k)
    rg = k.sbuf("rt_g", [128, 8], F32)
    ds2 = DSem(k, "rt_g")
    dve.wait(ds2.dma(sp, rg[:, :], cst["ret_g"][:, :]))
    oin = Ring(k, "rt_oin", 4, [128, TT], F32)
    gin = Ring(k, "rt_gin", 4, [128, TT], BF16)
    ytmp = k.sbuf("rt_y", [128, TT], F32)
    ob = OutStage(k, "rt_ob", 3, [128, TT], BF16)
    sq = norm_u.sq
    rstd = norm_u.rstd
    PSS = 7
    for hl in range(4):
        for t in range(NQT):
            cols = slice(t * TT, (t + 1) * TT)
            ld = []
            for c in range(2):
                r0 = hl * 256 + c * 128
                jo = oin.next()
                sp.wait(oin.free[jo])
                to_ = oin.ds[jo].dma(sp, oin.bufs[jo][:, :], Oraw[r0:r0 + 128, cols])
                jg = gin.next()
                sp.wait(gin.free[jg])
                tg_ = gin.ds[jg].dma(sp, gin.bufs[jg][:, :], I["RG"][r0:r0 + 128, cols])
                ld.append((jo, to_, jg, tg_))
            pe.wait(cm.ps_free[PSS])
            for c in range(2):
                jo, to_, jg, tg_ = ld[c]
                js = sq.next()
                act.wait(to_, sq.free[js])
                ta = act.m(nc.scalar.activation(out=sq.bufs[js][:, :], in_=oin.bufs[jo][:, :], func=AF.Square))
                pe.wait(ta)
                sq.free[js] = pe.m(nc.tensor.matmul(cm.ps[PSS][:, :], lhsT=cm.ones_f[:, :], rhs=sq.bufs[js][:, :],
                                                    start=(c == 0), stop=(c == 1)))
            act.wait(pe.last_tok, norm_u.rstd_free)
            ta = act.m(nc.scalar.activation(out=rstd[:, :], in_=cm.ps[PSS][:, :], func=AF.Sqrt, scale=1.0 / 256,
                                            bias=norm_u.eps_t[:, 0:1]))
            cm.ps_free[PSS] = ta
            dve.wait(ta)
            dve(nc.vector.reciprocal(out=rstd[:, :], in_=rstd[:, :]))
            for c in range(2):
                jo, to_, jg, tg_ = ld[c]
                r0 = hl * 256 + c * 128
                dve.wait(to_, tg_)
                dve(nc.vector.scalar_tensor_tensor(out=ytmp[:, :], in0=oin.bufs[jo][:, :], scalar=rg[:, hl * 2 + c:hl * 2 + c + 1],
                                                   in1=rstd[:, :], op0=ALU.mult, op1=ALU.mult))
                jb_, obuf = ob.slot(dve)
                ty = dve.m(nc.vector.tensor_tensor(out=obuf[:, :], in0=ytmp[:, :], in1=gin.bufs[jg][:, :], op=ALU.mult))
                oin.free[jo] = ty
                gin.free[jg] = ty
                ob.store(jb_, ty, [(Od[r0:r0 + 128, cols], obuf[:, :])])
            norm_u.rstd_free = dve.last_tok
    k.end_phase()


def phase_mixc(k, cm, hT, W, Oin):
    nc = k.nc
    pe, act, dve, sp = k.pe, k.act, k.dve, k.sp
    k.begin_phase()
    setup_norm(k)
    g_sb = k.sbuf("mc_g", [128, KC], F32)
    bg = k.sbuf("mc_bg", [128, 3, KC], F32)
    pds = DSem(k, "mc_p")
    pds.dma(sp, g_sb[:, :], W["mix_g"][:, :])
    tparam = pds.dma(sp, bg[:, :, :], W["bg"][:, :, :])
    dve.wait(tparam)
    act.wait(tparam)
    uT = k.sbuf("mc_u", [128, KC, TT], BF16)
    oT = k.sbuf("mc_o", [128, KC, TT], BF16)
    mg = k.sbuf("mc_m", [128, KC, TT], BF16)
    macc = [k.sbuf(f"mc_acc{i}", [128, TT], F32) for i in range(4)]
    mtmp = k.sbuf("mc_tmp", [128, TT], F32)
    stage = Ring(k, "mc_st", 3, [128, TT], F32)
    sa = Ring(k, "mc_sa", 4, [128, TT], F32, dma=False)
    res = Ring(k, "mc_res", 3, [128, TT], F32)
    ods = DSem(k, "mc_o")
    o_free = None
    mg_free = None
    ups = [(W["w_up_ret"], 16, 0), (W["w_up_fox"], 8, 16), (W["w_up_mla"], 8, 24)]
    pi = 0
    for t in range(NT):
        cols = slice(t * TT, (t + 1) * TT)
        tu = norm_u(k, cm, hT, t, g_sb, uT, stage)
        sp.wait(o_free)
        ods.dma(sp, oT[:, 0:16, :], Oin["ORc"][:, cols].rearrange("(c p) t -> p c t", p=128))
        ods.dma(sp, oT[:, 16:24, :], Oin["OFc"][:, cols].rearrange("(c p) t -> p c t", p=128))
        to_ = ods.dma(sp, oT[:, 24:32, :], Oin["OMc"][:, cols].rearrange("(c p) t -> p c t", p=128))
        pe.wait(tu, to_)
        for gq in range(D // 512):
            for i in range(3):
                wu, nku, off = ups[i]
                tgate = [None] * 4
                tup = [None] * 4
                for cc in range(4):
                    pe.wait(cm.ps_free[cc])
                for g in range(2):
                    jg, tlg, tkg = cm.wload([(lambda tl_: slab_view(tl_, 16, 512),
                                              wview(W["w_gate"][i], g * 16, 16, gq * 512, 512))])
                    vg = slab_view(tlg, 16, 512)
                    pe.wait(tkg)
                    for cc in range(4):
                        for kc in range(16):
                            mm = nc.tensor.matmul(cm.ps[cc][:, :], lhsT=vg[:, kc, cc * 128:(cc + 1) * 128],
                                                  rhs=uT[:, g * 16 + kc, :], start=(g == 0 and kc == 0),
                                                  stop=(g == 1 and kc == 15))
                        tgate[cc] = pe.m(mm)
                    cm.wfree(jg, tgate[3])
                for cc in range(4):
                    pe.wait(cm.ps_free[4 + cc])
                ju, tlu, tku = cm.wload([(lambda tl_: slab_view(tl_, nku, 512), wview(wu, 0, nku, gq * 512, 512))])
                vu = slab_view(tlu, nku, 512)
                pe.wait(tku)
                for cc in range(4):
                    for kc in range(nku):
                        mm = nc.tensor.matmul(cm.ps[4 + cc][:, :], lhsT=vu[:, kc, cc * 128:(cc + 1) * 128],
                                              rhs=oT[:, off + kc, :], start=(kc == 0), stop=(kc == nku - 1))
                    tup[cc] = pe.m(mm)
                cm.wfree(ju, tup[3])
                for cc in range(4):
                    n = gq * 4 + cc
                    js = sa.next()
                    act.wait(tgate[cc], sa.free[js])
                    tsa = act.m(nc.scalar.activation(out=sa.bufs[js][:, :], in_=cm.ps[cc][:, :], func=AF.Sigmoid,
                                                     bias=bg[:, i, n:n + 1]))
                    cm.ps_free[cc] = tsa
                    dve.wait(tsa, tup[cc])
                    if i == 0:
                        th = dve.m(nc.vector.tensor_tensor(out=macc[cc][:, :], in0=cm.ps[4 + cc][:, :],
                                                           in1=sa.bufs[js][:, :], op=ALU.mult))
                    else:
                        th = dve.m(nc.vector.tensor_tensor(out=mtmp[:, :], in0=cm.ps[4 + cc][:, :],
                                                           in1=sa.bufs[js][:, :], op=ALU.mult))
                        if i == 1:
                            dve(nc.vector.tensor_tensor(out=macc[cc][:, :], in0=macc[cc][:, :], in1=mtmp[:, :], op=ALU.add))
                        else:
                            if n == 0:
                                dve.wait(mg_free)
                            dve(nc.vector.tensor_tensor(out=mg[:, n, :], in0=macc[cc][:, :], in1=mtmp[:, :], op=ALU.add))
                    sa.free[js] = th
                    cm.ps_free[4 + cc] = th
        norm_u.u_free = pe.last_tok
        o_free = pe.last_tok
        tmg = dve.tail()
        pe.wait(tmg)
        def epi_out(n, ps, tok):
            j = stage.next()
            sp.wait(stage.free[j])
            tk = stage.ds[j].dma(sp, stage.bufs[j][:, :], hT[n * 128:(n + 1) * 128, cols])
            jr = res.next()
            dve.wait(tk, tok, res.ds[jr].tok())
            tr = dve.m(nc.vector.tensor_tensor(out=res.bufs[jr][:, :], in0=ps[:, :], in1=stage.bufs[j][:, :], op=ALU.add))
            stage.free[j] = tr
            sp.wait(tr)
            res.ds[jr].dma(sp, hT[n * 128:(n + 1) * 128, cols], res.bufs[jr][:, :])
            return tr
        linear_fm(k, cm, lambda kc: mg[:, kc, :], KC, W["w_out"], 0, D, epi_out)
        mg_free = pe.last_tok
    k.end_phase()


def phase_final(k, cm, hT, gain, out):
    nc = k.nc
    pe, act, dve, sp = k.pe, k.act, k.dve, k.sp
    k.begin_phase()
    setup_norm(k)
    g_sb = k.sbuf("fn_g", [128, KC], F32)
    gds = DSem(k, "fn_g")
    dve.wait(gds.dma(sp, g_sb[:, :], gain[:, :]))
    stage = Ring(k, "fn_st", 3, [128, TT], F32)
    yb = Ring(k, "fn_y", 8, [128, TT], F32, dma=False)
    ost = OutStage(k, "fn_o", 4, [128, TT], F32)
    for t in range(NT):
        pend = []

        def cb(c, st_buf, rstd):
            jy = yb.next()
            dve.wait(yb.free[jy])
            ty = dve.m(nc.vector.scalar_tensor_tensor(out=yb.bufs[jy][:, :], in0=st_buf[:, :], scalar=g_sb[:, c:c + 1],
                                                      in1=rstd[:, :], op0=ALU.mult, op1=ALU.mult))
            pe.wait(ty)
            if c % 4 == 0:
                for tb in range(4):
                    pe.wait(cm.ps_free[tb])
            for tb in range(4):
                mm = nc.tensor.matmul(cm.ps[tb][:, (c % 4) * 128:(c % 4 + 1) * 128], lhsT=yb.bufs[jy][:, tb * 128:(tb + 1) * 128],
                                      rhs=cm.ident[:, :], start=True, stop=True)
            tp = pe.m(mm)
            yb.free[jy] = tp
            if c % 4 == 3:
                cg = c // 4
                for tb in range(4):
                    eng = act if tb % 2 == 0 else dve
                    jo, obuf = ost.slot(eng)
                    eng.wait(tp)
                    if eng is act:
                        te = act.m(nc.scalar.copy(out=obuf[:, :], in_=cm.ps[tb][:, :]))
                    else:
                        te = dve.m(nc.vector.tensor_copy(out=obuf[:, :], in_=cm.ps[tb][:, :]))
                    cm.ps_free[tb] = te
                    ost.store(jo, te, [(out[t * TT + tb * 128:t * TT + (tb + 1) * 128, cg * 512:(cg + 1) * 512], obuf[:, :])])
            return ty
        norm_u(k, cm, hT, t, g_sb, None, stage, out_f32=cb)
    k.end_phase()


def phase_copy(k, cm, src, dst):
    ds = DSem(k, "cp")
    n = 8
    rows = src.shape[0] // n
    for i in range(n):
        ds.dma(k.sp, dst[i * rows:(i + 1) * rows, :], src[i * rows:(i + 1) * rows, :])
    k.barrier()


BUNDLE = {
    "RQ": ([2, 4, 128, T], BF16), "RK": ([2, 4, 128, T], BF16), "RV": ([2, T, 1024], BF16),
    "RG": ([2, 1024, T], BF16), "FQ": ([2, 4, 128, T], BF16), "FK": ([2, 4, 128, T], BF16),
    "FV": ([2, T, 512], BF16), "FL": ([T, 8], F32), "MQN": ([2, 4, 128, T], BF16),
    "MQP": ([2, 4, 64, T], BF16), "MKN": ([2, 4, 128, T], BF16), "MV": ([2, T, 512], BF16),
    "MKP": ([64, T], BF16),
}


def decl_ffn(k, pre):
    return (k.dt(pre + "_g", [128, KC], F32, "ExternalInput"),
            k.dt(pre + "_w13", [D, 2 * DFF], F32, "ExternalInput"),
            k.dt(pre + "_w2", [DFF, D], F32, "ExternalInput"))


def decl_mixa(k):
    W = {"mix_g": k.dt("mix_g", [128, KC], F32, "ExternalInput"),
         "w_in": k.dt("w_in", [D, INW], F32, "ExternalInput"),
         "bfg": k.dt("bfg", [128, 8], F32, "ExternalInput"),
         "qn_g": k.dt("qn_g", [128, 8], F32, "ExternalInput"),
         "kvn_g": k.dt("kvn_g", [128, 4], F32, "ExternalInput"),
         "w_uq": k.dt("w_uq", [1024, 1536], F32, "ExternalInput"),
         "w_ukv": k.dt("w_ukv", [512, 2048], F32, "ExternalInput")}
    pos = k.dt("pos", [1, T], I32, "ExternalInput")
    inv = k.dt("c_inv", [128, 2], F32, "ExternalInput")
    tabs = {n: k.dt("tab_" + n, [128, T], F32, "Internal") for n in ("cosR", "sinR", "cosM", "sinM")}
    O = {n: k.dt("o_" + n, sh, dt_, "ExternalOutput") for n, (sh, dt_) in BUNDLE.items()}
    return W, pos, inv, tabs, O


MIXB_IN = {
    "RQ": ([4, 128, S], BF16), "RK": ([4, 128, S], BF16), "RV": ([S, 1024], BF16), "RG": ([1024, S], BF16),
    "FQ": ([4, 128, S], BF16), "FK": ([4, 128, S], BF16), "FV": ([S, 512], BF16), "FL": ([S, 4], F32),
    "MQN": ([4, 128, S], BF16), "MQP": ([4, 64, S], BF16), "MKN": ([4, 128, S], BF16), "MV": ([S, 512], BF16),
    "MKP": ([64, S], BF16),
}
MIXB_CST = {
    "tri": ([128, 128], BF16), "cmask": ([128, 128], BF16), "U": ([128, 128], F32), "M": ([128, 128], F32),
    "DmT": ([128, 4, 64], F32), "xi": ([128, 4, 64], F32), "zeta": ([128, 4], F32), "gC": ([128, 4], F32),
    "ret_g": ([128, 8], F32),
}


def mixb_consts(half, ret_norm_l):
    bf = ml_dtypes.bfloat16
    p = np.arange(128)
    tri = (p[:, None] <= p[None, :]).astype(np.float32)
    cmask = ((p[:, None] // 64) <= (p[None, :] // 64)).astype(np.float32)
    jh = np.arange(128)
    Mm = ((jh[:, None] % 4 == jh[None, :] % 4) & (jh[:, None] // 4 < jh[None, :] // 4)).astype(np.float32)
    heads = half * 4 + np.arange(4)
    log_g = np.log1p(-np.exp2(-5.0 - heads.astype(np.float64)))
    idx = np.arange(64, dtype=np.float64)
    dm = np.exp(log_g[:, None, None] * np.abs(idx[:, None] - idx[None, :])) * (128.0 ** -0.5)
    DmT = np.tile(dm.transpose(1, 0, 2), (2, 1, 1))
    xi = np.exp(log_g[:, None] * (idx + 1.0)) * (128.0 ** -0.5)
    xi = np.tile(xi[None], (128, 1, 1))
    zeta = np.exp(log_g[None, :] * (63.0 - (p % 64))[:, None])
    gC = np.tile(np.exp(log_g * 64.0)[None, :], (128, 1))
    rg = np.asarray(ret_norm_l).reshape(8, 2, 128)[half * 4:half * 4 + 4]
    ret_g = np.ascontiguousarray(rg.transpose(2, 0, 1).reshape(128, 8))
    return {"tri": tri.astype(bf), "cmask": cmask.astype(bf), "U": tri, "M": Mm,
            "DmT": DmT.astype(np.float32), "xi": xi.astype(np.float32), "zeta": zeta.astype(np.float32),
            "gC": gC.astype(np.float32), "ret_g": ret_g.astype(np.float32)}


def build_mixb(parts=("ret", "fox", "mla")):
    k = K()
    cst0 = {"ident": k.dt("c_ident", [128, 128], F32, "ExternalInput")}
    cm = Common(k, cst0)
    I = {n: k.dt("b_" + n, sh, dt_, "ExternalInput") for n, (sh, dt_) in MIXB_IN.items()}
    cst = {n: k.dt("cb_" + n, sh, dt_, "ExternalInput") for n, (sh, dt_) in MIXB_CST.items()}
    OR = k.dt("o_OR", [1024, S], BF16, "ExternalOutput")
    OF = k.dt("o_OF", [512, S], BF16, "ExternalOutput")
    OM = k.dt("o_OM", [512, S], BF16, "ExternalOutput")
    Oraw = k.dt("oraw", [1024, S], F32, "Internal")
    if "fox" in parts:
        phase_fox(k, cm, I, cst, OF)
    if "mla" in parts:
        phase_mla(k, cm, I, cst, OM)
    if "ret" in parts:
        phase_ret(k, cm, I, cst, Oraw, OR)
    k.barrier()
    return k


def decl_mixc(k):
    W = {"mix_g": k.dram.get("mix_g") if "mix_g" in k.dram else k.dt("mix_g", [128, KC], F32, "ExternalInput"),
         "bg": k.dt("bg", [128, 3, KC], F32, "ExternalInput"),
         "w_gate": k.dt("w_gate", [3, D, D], F32, "ExternalInput"),
         "w_up_ret": k.dt("w_up_ret", [2048, D], F32, "ExternalInput"),
         "w_up_fox": k.dt("w_up_fox", [1024, D], F32, "ExternalInput"),
         "w_up_mla": k.dt("w_up_mla", [1024, D], F32, "ExternalInput"),
         "w_out": k.dt("w_out", [D, D], F32, "ExternalInput")}
    Oin = {"ORc": k.dt("ORc", [2048, T], BF16, "ExternalInput"),
           "OFc": k.dt("OFc", [1024, T], BF16, "ExternalInput"),
           "OMc": k.dt("OMc", [1024, T], BF16, "ExternalInput")}
    return W, Oin


def build(kind):
    k = K()
    cst = {"ident": k.dt("c_ident", [128, 128], F32, "ExternalInput")}
    cm = Common(k, cst)
    hT = k.dt("hT", [D, T], F32, "ExternalOutput")
    if kind == "TEST_FFN":
        x = k.dt("x", [T, D], F32, "ExternalInput")
        phase_p0(k, cm, x, hT)
        g, w13, w2 = decl_ffn(k, "ffn1")
        phase_ffn(k, cm, hT, g, w13, w2)
        k.barrier()
        return k
    if kind == "A0":
        x = k.dt("x", [T, D], F32, "ExternalInput")
        phase_p0(k, cm, x, hT)
    else:
        hin = k.dt("hT_in", [D, T], F32, "ExternalInput")
        phase_copy(k, cm, hin, hT)
    if kind in ("CM", "C1"):
        W, Oin = decl_mixc(k)
        phase_mixc(k, cm, hT, W, Oin)
        g, w13, w2 = decl_ffn(k, "ffn2")
        phase_ffn(k, cm, hT, g, w13, w2)
    if kind in ("A0", "A1"):
        g, w13, w2 = decl_ffn(k, "ffn1")
        phase_ffn(k, cm, hT, g, w13, w2)
        W = {"mix_g": k.dt("mix_g", [128, KC], F32, "ExternalInput"),
             "w_in": k.dt("w_in", [D, INW], F32, "ExternalInput"),
             "bfg": k.dt("bfg", [128, 8], F32, "ExternalInput"),
             "qn_g": k.dt("qn_g", [128, 8], F32, "ExternalInput"),
             "kvn_g": k.dt("kvn_g", [128, 4], F32, "ExternalInput"),
             "w_uq": k.dt("w_uq", [1024, 1536], F32, "ExternalInput"),
             "w_ukv": k.dt("w_ukv", [512, 2048], F32, "ExternalInput")}
        pos = k.dt("pos", [1, T], I32, "ExternalInput")
        inv = k.dt("c_inv", [128, 2], F32, "ExternalInput")
        tabs = {n: k.dt("tab_" + n, [128, T], F32, "Internal") for n in ("cosR", "sinR", "cosM", "sinM")}
        O = {n: k.dt("o_" + n, sh, dt_, "ExternalOutput") for n, (sh, dt_) in BUNDLE.items()}
        phase_tables(k, cm, pos, inv, tabs)
        phase_mixa(k, cm, hT, W, tabs, O)
    if kind == "C1":
        fg = k.dt("final_g", [128, KC], F32, "ExternalInput")
        out = k.dt("out", [T, D], F32, "ExternalOutput")
        phase_final(k, cm, hT, fg, out)
    k.barrier()
    return k


def pvec(v, n=128):
    v = np.asarray(v)
    return np.ascontiguousarray(v.reshape(-1, n).T)


NCORES = 8
_CACHE = {}


def _prog(kind):
    if kind not in _CACHE:
        _CACHE[kind] = build_mixb() if kind == "B" else build(kind)
    return _CACHE[kind]


def _rope_inv():
    inv64 = (10000.0 ** (-np.arange(64, dtype=np.float32) / 64)).astype(np.float32)
    inv32 = (10000.0 ** (-np.arange(32, dtype=np.float32) / 32)).astype(np.float32)
    p = np.arange(128)
    return np.ascontiguousarray(np.stack([inv64[p % 64], inv32[p % 32]], axis=1).astype(np.float32))


def _run(kind, in_maps):
    k = _prog(kind)
    res = run_bass_kernel_spmd(k.nc, in_maps, core_ids=list(range(NCORES)))
    return res.results


def _mixa_inputs(inp, l, pre=""):
    return {pre + "mix_g": pvec(inp["mix_norm"][l]), "w_in": np.asarray(inp["w_in"][l]),
            "bfg": np.ascontiguousarray(np.broadcast_to(np.asarray(inp["b_forget"][l])[None, :], (128, 8))),
            "qn_g": pvec(inp["mla_q_norm"][l]), "kvn_g": pvec(inp["mla_kv_norm"][l]),
            "w_uq": np.asarray(inp["w_uq"][l]), "w_ukv": np.asarray(inp["w_ukv"][l]), "c_inv": _rope_inv()}


def _ffn_inputs(inp, l, which):
    return {which + "_g": pvec(inp[which + "_norm"][l]), which + "_w13": np.asarray(inp[which + "_w13"][l]),
            which + "_w2": np.asarray(inp[which + "_w2"][l])}


def _mixc_inputs(inp, l):
    bgt = np.asarray(inp["b_gate"][l]).reshape(3, KC, 128).transpose(2, 0, 1)
    return {"mix_g": pvec(inp["mix_norm"][l]), "bg": np.ascontiguousarray(bgt), "w_gate": np.asarray(inp["w_gate"][l]),
            "w_up_ret": np.asarray(inp["w_up_ret"][l]), "w_up_fox": np.asarray(inp["w_up_fox"][l]),
            "w_up_mla": np.asarray(inp["w_up_mla"][l]), "w_out": np.asarray(inp["w_out"][l])}


def _regroup_b(resA, inp, l):
    maps = []
    ident = np.eye(128, dtype=np.float32)
    for c in range(NCORES):
        b, half = c // 2, c % 2
        r0, r1 = resA[2 * b], resA[2 * b + 1]
        m = {"c_ident": ident}
        for n in ("RQ", "RK", "FQ", "FK", "MQN", "MQP", "MKN"):
            m["b_" + n] = np.concatenate([r0["o_" + n][half], r1["o_" + n][half]], axis=-1)
        for n in ("RV", "FV", "MV"):
            m["b_" + n] = np.concatenate([r0["o_" + n][half], r1["o_" + n][half]], axis=0)
        m["b_RG"] = np.concatenate([r0["o_RG"][half], r1["o_RG"][half]], axis=-1)
        m["b_MKP"] = np.concatenate([r0["o_MKP"], r1["o_MKP"]], axis=-1)
        fl = np.concatenate([r0["o_FL"], r1["o_FL"]], axis=0)
        m["b_FL"] = np.ascontiguousarray(fl[:, half * 4:half * 4 + 4])
        for n, v in mixb_consts(half, inp["ret_norm"][l]).items():
            m["cb_" + n] = v
        maps.append(m)
    return maps


def _regroup_c(resB):
    outs = []
    for c in range(NCORES):
        b, half = c // 2, c % 2
        r0, r1 = resB[2 * b], resB[2 * b + 1]
        sl = slice(half * T, (half + 1) * T)
        outs.append({"ORc": np.ascontiguousarray(np.concatenate([r0["o_OR"][:, sl], r1["o_OR"][:, sl]], axis=0)),
                     "OFc": np.ascontiguousarray(np.concatenate([r0["o_OF"][:, sl], r1["o_OF"][:, sl]], axis=0)),
                     "OMc": np.ascontiguousarray(np.concatenate([r0["o_OM"][:, sl], r1["o_OM"][:, sl]], axis=0))})
    return outs


def kernel(**inputs):
    inp = inputs
    x = np.asarray(inp["x"])
    positions = np.asarray(inp["positions"])
    ident = np.eye(128, dtype=np.float32)

    def pos_of(c):
        b, half = c // 2, c % 2
        return np.ascontiguousarray(positions[b, half * T:(half + 1) * T][None, :].astype(np.int32))

    hTs = None
    out = np.empty((4, S, D), np.float32)
    for l in range(2):
        shared = {"c_ident": ident}
        shared.update(_ffn_inputs(inp, l, "ffn1"))
        shared.update(_mixa_inputs(inp, l))
        maps = []
        for c in range(NCORES):
            b, half = c // 2, c % 2
            m = dict(shared)
            if l == 0:
                m["x"] = np.ascontiguousarray(x[b, half * T:(half + 1) * T])
            else:
                m["hT_in"] = hTs[c]
            m["pos"] = pos_of(c)
            maps.append(m)
        resA = _run("A0" if l == 0 else "A1", maps)
        hTs = [r["hT"] for r in resA]
        del maps, shared
        resB = _run("B", _regroup_b(resA, inp, l))
        oc = _regroup_c(resB)
        del resA, resB
        shared = {"c_ident": ident}
        shared.update(_mixc_inputs(inp, l))
        shared.update(_ffn_inputs(inp, l, "ffn2"))
        if l == 1:
            shared["final_g"] = pvec(inp["final_norm"])
        maps = []
        for c in range(NCORES):
            m = dict(shared)
            m["hT_in"] = hTs[c]
            m.update(oc[c])
            maps.append(m)
        resC = _run("CM" if l == 0 else "C1", maps)
        hTs = [r["hT"] for r in resC]
        del maps, shared
        if l == 1:
            for c in range(NCORES):
                b, half = c // 2, c % 2
                out[b, half * T:(half + 1) * T] = resC[c]["out"]
    return out
```
